# Optimizing a Trainium2 kernel written in Bass

```python
import jax, jax.numpy as jnp
from jax import lax
import numpy as np

D_MODEL = 1024
BATCH = 8
SEQ = 4096
DEPTH = 4

GRID_W = 64
CTX_LEN = 256
N_MIXERS = 2
N_HEADS = 16
HEAD_DIM = D_MODEL // N_HEADS
N_DIRS = 2
DECAY_LORA = 64
AAA_LORA = 64
MV_LORA = 32
GATE_LORA = 128
FOURIER_GROUPS = 4
MLP_HIDDEN = 4 * D_MODEL
N_MOD = 6
RMS_EPS = 1e-6
LNX_EPS = 64e-5
L2_EPS = 1e-24

kernel_name = 'hybrid_rwkv7_fnet_adaln_block'


def _rmsnorm(x, g):
    x32 = x.astype(jnp.float32)
    y = x32 * lax.rsqrt(jnp.mean(x32 * x32, axis=-1, keepdims=True) + RMS_EPS)
    return (y * g.astype(jnp.float32)).astype(x.dtype)


def _modulate(h, shift, scale):
    return h * (1 + scale) + shift


def _mlp(h, w1, w2):
    return jnp.square(jax.nn.relu(h @ w1)) @ w2


def _heads(t):
    return t.reshape(t.shape[0], t.shape[1], N_HEADS, HEAD_DIM)


def _qshift(h):
    b, t, d = h.shape
    rows = t // GRID_W
    g = h.reshape(b, rows, GRID_W, 4, d // 4)
    left = jnp.pad(g[:, :, :-1, 0], ((0, 0), (0, 0), (1, 0), (0, 0)))
    right = jnp.pad(g[:, :, 1:, 1], ((0, 0), (0, 0), (0, 1), (0, 0)))
    up = jnp.pad(g[:, :-1, :, 2], ((0, 0), (1, 0), (0, 0), (0, 0)))
    down = jnp.pad(g[:, 1:, :, 3], ((0, 0), (0, 1), (0, 0), (0, 0)))
    return jnp.stack([left, right, up, down], axis=3).reshape(b, t, d)


def _shift_ctx(h):
    d = h.shape[-1]
    prev = jnp.pad(h[:, :-1, : d // 2], ((0, 0), (1, 0), (0, 0)))
    nxt = jnp.pad(h[:, 1:, d // 2:], ((0, 0), (0, 1), (0, 0)))
    return jnp.concatenate([prev, nxt], axis=-1)


def _wkv_scan(r, v, w, k, a, b, s0, reverse):
    dt = v.dtype
    with_out = r is not None
    ins = (w, k, v, a, b) + ((r,) if with_out else ())
    xs = tuple(jnp.swapaxes(t.astype(jnp.float32), 0, 1) for t in ins)

    def step(s, inp):
        w_t, k_t, v_t, a_t, b_t = inp[:5]
        sa = jnp.einsum('bhvk,bhk->bhv', s, a_t)
        s = s * w_t[:, :, None, :] + sa[..., None] * b_t[:, :, None, :] + v_t[..., None] * k_t[:, :, None, :]
        y = jnp.einsum('bhvk,bhk->bhv', s, inp[5]) if with_out else None
        return s, y

    s_fin, ys = lax.scan(step, s0, xs, reverse=reverse)
    if with_out:
        ys = jnp.swapaxes(ys, 0, 1).astype(dt)
    return ys, s_fin


def _rwkv_streams(h, shifted, lp, vres, v_first, readout):
    xx = shifted - h
    mu = lp['mu']
    xr, xw, xk, xv, xa, xg = (h + xx * mu[j] for j in range(6))
    k = xk @ lp['wk']
    v = xv @ lp['wv']
    if vres is not None:
        v0, v1, v2 = vres
        v = v + (v_first - v) * jax.nn.sigmoid(v0 + (xv @ v1) @ v2)
    kk = _heads(k * lp['kk']).astype(jnp.float32)
    kk = kk * lax.rsqrt(jnp.maximum(jnp.sum(kk * kk, axis=-1, keepdims=True), L2_EPS))
    dirs = []
    for d in range(N_DIRS):
        w_log = -jax.nn.softplus(-(lp['w0'][d] + jnp.tanh(xw @ lp['w1'][d]) @ lp['w2'][d])) - 0.5
        decay = jnp.exp(-jnp.exp(w_log.astype(jnp.float32)))
        iclr = jax.nn.sigmoid(lp['a0'][d] + (xa @ lp['a1'][d]) @ lp['a2'][d])
        k_d = k * (1 + (iclr - 1) * lp['ka'])
        dirs.append((_heads(decay), _heads(k_d), -kk, kk * _heads(iclr).astype(jnp.float32)))
    st = {'v': v, 'dirs': dirs}
    if readout:
        st['r'] = _heads(xr @ lp['wr'])
        st['g'] = jax.nn.sigmoid(xg @ lp['g1']) @ lp['g2']
    return st


def _rwkv_readout(y, st, lp):
    b, l = y.shape[:2]
    y32 = y.astype(jnp.float32)
    mean = jnp.mean(y32, axis=-1, keepdims=True)
    var = jnp.mean(jnp.square(y32 - mean), axis=-1, keepdims=True)
    yn = ((y32 - mean) * lax.rsqrt(var + LNX_EPS)).reshape(b, l, D_MODEL) * lp['lnx_w'] + lp['lnx_b']
    r = st['r']
    vh = _heads(st['v'])
    bonus = sum(jnp.sum(r * kd * lp['rk'], axis=-1, keepdims=True) for (_, kd, _, _) in st['dirs']) * vh
    out = (yn.astype(st['v'].dtype) + bonus.reshape(b, l, D_MODEL)) * st['g']
    return out @ lp['wo']


def _rwkv_mixer(h_ctx, h_lat, lp, vres, vf_ctx, vf_lat, ctx_readout):
    st_c = _rwkv_streams(h_ctx, _shift_ctx(h_ctx), lp, vres, vf_ctx, ctx_readout)
    st_l = _rwkv_streams(h_lat, _qshift(h_lat), lp, vres, vf_lat, True)
    vh_c = _heads(st_c['v'])
    vh_l = _heads(st_l['v'])
    s0 = jnp.zeros((h_lat.shape[0], N_HEADS, HEAD_DIM, HEAD_DIM), jnp.float32)
    y_ctx = []
    y_lat = []
    for d in range(N_DIRS):
        rev = d == 1
        yc, s_ctx = _wkv_scan(st_c.get('r'), vh_c, *st_c['dirs'][d], s0, rev)
        yl, _ = _wkv_scan(st_l['r'], vh_l, *st_l['dirs'][d], s_ctx, rev)
        y_lat.append(yl)
        y_ctx.append(yc)
    o_lat = _rwkv_readout(y_lat[0] + y_lat[1], st_l, lp)
    o_ctx = _rwkv_readout(y_ctx[0] + y_ctx[1], st_c, lp) if ctx_readout else None
    return o_ctx, o_lat, st_c['v'], st_l['v']


def _fourier_mixer(h, wo):
    b, l, d = h.shape
    g = h.astype(jnp.float32).reshape(b, l, FOURIER_GROUPS, d // FOURIER_GROUPS)
    f = jnp.fft.fftn(g, axes=(1, 3), norm='ortho').real
    return f.reshape(b, l, d).astype(h.dtype) @ wo


def setup_inputs(seed: int = 0) -> dict:
    key = jax.random.key(seed)
    ks = iter(jax.random.split(key, 40))
    D = D_MODEL
    F = MLP_HIDDEN
    n_a = (DEPTH + 1) // 2
    n_b = DEPTH // 2
    n_v = n_a - 1

    def nrm(shape, s):
        return jax.random.normal(next(ks), shape, jnp.float32) * s

    def unif(shape, lo, hi):
        return jax.random.uniform(next(ks), shape, jnp.float32, minval=lo, maxval=hi)

    return {
        'x': nrm((BATCH, SEQ, D), 1.0),
        'c': nrm((BATCH, D), 1.0),
        'ctx': nrm((BATCH, CTX_LEN, D), 1.0),
        'c_ctx': nrm((D,), 1.0),
        'norm1_g': 1.0 + nrm((DEPTH, D), 0.02),
        'norm2_g': 1.0 + nrm((DEPTH, D), 0.02),
        'mod_w': nrm((DEPTH, D, N_MOD * D), 0.5 * D ** -0.5),
        'mod_b': nrm((DEPTH, N_MOD * D), 0.02),
        'mlp_w1': nrm((DEPTH, D, F), D ** -0.5),
        'mlp_w2': nrm((DEPTH, F, D), F ** -0.5),
        'rk_mu': unif((n_a, 6, D), 0.0, 1.0),
        'rk_wr': nrm((n_a, D, D), D ** -0.5),
        'rk_wk': nrm((n_a, D, D), D ** -0.5),
        'rk_wv': nrm((n_a, D, D), D ** -0.5),
        'rk_wo': nrm((n_a, D, D), D ** -0.5),
        'rk_w0': unif((n_a, N_DIRS, D), -3.0, 0.5),
        'rk_w1': nrm((n_a, N_DIRS, D, DECAY_LORA), D ** -0.5),
        'rk_w2': nrm((n_a, N_DIRS, DECAY_LORA, D), 0.5 * DECAY_LORA ** -0.5),
        'rk_a0': nrm((n_a, N_DIRS, D), 0.1),
        'rk_a1': nrm((n_a, N_DIRS, D, AAA_LORA), D ** -0.5),
        'rk_a2': nrm((n_a, N_DIRS, AAA_LORA, D), 0.5 * AAA_LORA ** -0.5),
        'rk_v0': 0.5 + nrm((n_v, D), 0.1),
        'rk_v1': nrm((n_v, D, MV_LORA), D ** -0.5),
        'rk_v2': nrm((n_v, MV_LORA, D), 0.5 * MV_LORA ** -0.5),
        'rk_g1': nrm((n_a, D, GATE_LORA), D ** -0.5),
        'rk_g2': nrm((n_a, GATE_LORA, D), GATE_LORA ** -0.5),
        'rk_kk': 0.85 + nrm((n_a, D), 0.02),
        'rk_ka': 1.0 + nrm((n_a, D), 0.02),
        'rk_rk': nrm((n_a, N_HEADS, HEAD_DIM), 0.1),
        'rk_lnx_w': 1.0 + nrm((n_a, D), 0.02),
        'rk_lnx_b': nrm((n_a, D), 0.02),
        'ft_wo': nrm((n_b, D, D), D ** -0.5),
        'final_g': 1.0 + nrm((D,), 0.02),
    }


def reference(x, c, ctx, c_ctx, norm1_g, norm2_g, mod_w, mod_b, mlp_w1, mlp_w2, rk_mu, rk_wr, rk_wk, rk_wv, rk_wo, rk_w0, rk_w1, rk_w2, rk_a0, rk_a1, rk_a2, rk_v0, rk_v1, rk_v2, rk_g1, rk_g2, rk_kk, rk_ka, rk_rk, rk_lnx_w, rk_lnx_b, ft_wo, final_g):
    last_a = ((DEPTH - 1) // N_MIXERS) * N_MIXERS
    silu_c = jax.nn.silu(c)
    silu_cc = jax.nn.silu(c_ctx)
    vf_ctx = None
    vf_lat = None
    for i in range(DEPTH):
        mixer = i % N_MIXERS
        idx = i // N_MIXERS
        ctx_in = i <= last_a
        ctx_out = i < last_a
        sh1, sc1, ga1, sh2, sc2, ga2 = jnp.split((silu_c @ mod_w[i] + mod_b[i])[:, None, :], N_MOD, axis=-1)
        h_lat = _modulate(_rmsnorm(x, norm1_g[i]), sh1, sc1)
        if ctx_in:
            csh1, csc1, cga1, csh2, csc2, cga2 = jnp.split(silu_cc @ mod_w[i] + mod_b[i], N_MOD, axis=-1)
            h_ctx = _modulate(_rmsnorm(ctx, norm1_g[i]), csh1, csc1)
        if mixer == 0:
            lp = {'mu': rk_mu[idx], 'wr': rk_wr[idx], 'wk': rk_wk[idx], 'wv': rk_wv[idx], 'wo': rk_wo[idx],
                  'w0': rk_w0[idx], 'w1': rk_w1[idx], 'w2': rk_w2[idx],
                  'a0': rk_a0[idx], 'a1': rk_a1[idx], 'a2': rk_a2[idx],
                  'g1': rk_g1[idx], 'g2': rk_g2[idx], 'kk': rk_kk[idx], 'ka': rk_ka[idx], 'rk': rk_rk[idx],
                  'lnx_w': rk_lnx_w[idx], 'lnx_b': rk_lnx_b[idx]}
            vres = (rk_v0[idx - 1], rk_v1[idx - 1], rk_v2[idx - 1]) if idx > 0 else None
            o_ctx, o_lat, v_c, v_l = _rwkv_mixer(h_ctx, h_lat, lp, vres, vf_ctx, vf_lat, ctx_out)
            if idx == 0:
                vf_ctx, vf_lat = v_c, v_l
        else:
            o_lat = _fourier_mixer(h_lat, ft_wo[idx])
            o_ctx = _fourier_mixer(h_ctx, ft_wo[idx]) if ctx_out else None
        x = x + ga1 * o_lat
        x = x + ga2 * _mlp(_modulate(_rmsnorm(x, norm2_g[i]), sh2, sc2), mlp_w1[i], mlp_w2[i])
        if ctx_out:
            ctx = ctx + cga1 * o_ctx
            ctx = ctx + cga2 * _mlp(_modulate(_rmsnorm(ctx, norm2_g[i]), csh2, csc2), mlp_w1[i], mlp_w2[i])
    return _rmsnorm(x, final_g)
```

```python
import math
import numpy as np
import ml_dtypes
from contextlib import ExitStack
import concourse.bass as bass
import concourse.mybir as mybir
from concourse.bass_utils import run_bass_kernel_spmd

F32 = mybir.dt.float32
BF16 = mybir.dt.bfloat16
AF = mybir.ActivationFunctionType
ALU = mybir.AluOpType
AX = mybir.AxisListType

D = 1024
NK = 8
NH = 16
C0 = math.exp(-0.5)
NMOD = 6


class Buf:
    __slots__ = ("lw", "rd", "t")

    def __init__(self, t=None):
        self.lw = None
        self.rd = {}
        self.t = t

    def __getitem__(self, k):
        return self.t[k]


class Sched:
    NDMA = 24

    def __init__(self, nc, stack):
        self.nc = nc
        self.names = ["tensor", "vector", "scalar", "gpsimd", "sync"]
        self.sem = {n: stack.enter_context(nc.semaphore("s_" + n)) for n in self.names}
        self.cnt = {n: 0 for n in self.names}
        self.prog = {n: [] for n in self.names}
        self.waited = {n: {} for n in self.names}
        self.dsem = [stack.enter_context(nc.semaphore("d%d" % i)) for i in range(self.NDMA)]
        self.dcnt = [0] * self.NDMA
        self.dnext = 0
        self.ninst = 0

    def _semobj(self, key):
        return self.sem[key] if isinstance(key, str) else self.dsem[key]

    def _collect(self, eng, reads, writes):
        need = {}

        def add(k, v):
            if k == eng and eng == "tensor":
                return
            if need.get(k, 0) < v:
                need[k] = v
        for t in reads:
            if t.lw is not None:
                add(*t.lw)
        for t in writes:
            if t.lw is not None:
                add(*t.lw)
            for k, v in t.rd.items():
                if k != eng:
                    add(k, v)
        w = self.waited[eng]
        out = []
        for k, v in need.items():
            if w.get(k, 0) >= v:
                continue
            w[k] = v
            out.append((k, v))
        return out

    def op(self, eng, fn, reads=(), writes=()):
        waits = self._collect(eng, reads, writes)
        self.cnt[eng] += 1
        v = self.cnt[eng]
        self.prog[eng].append((waits, fn, (eng, 1)))
        self.ninst += 1 + len(waits)
        for t in reads:
            t.rd[eng] = v
        for t in writes:
            t.lw = (eng, v)
            t.rd = {}

    def dma(self, q, out_ap, in_ap, reads=(), writes=()):
        slot = self.dnext
        self.dnext = (self.dnext + 1) % self.NDMA
        waits = self._collect(q, reads, writes)
        prev = self.dcnt[slot]
        w = self.waited[q]
        if prev > 0 and w.get(slot, 0) < prev:
            w[slot] = prev
            waits.append((slot, prev))
        self.dcnt[slot] += 16
        v = self.dcnt[slot]
        self.prog[q].append((waits, lambda e: e.dma_start(out=out_ap, in_=in_ap), (slot, 16)))
        self.ninst += 1 + len(waits)
        for t in reads:
            t.rd[slot] = v
        for t in writes:
            t.lw = (slot, v)
            t.rd = {}

    def barrier(self):
        for n in self.names:
            waits = []
            w = self.waited[n]
            for slot in range(self.NDMA):
                if self.dcnt[slot] > w.get(slot, 0):
                    w[slot] = self.dcnt[slot]
                    waits.append((slot, self.dcnt[slot]))
            for m in self.names:
                if m != n and self.cnt[m] > w.get(m, 0):
                    w[m] = self.cnt[m]
                    waits.append((m, self.cnt[m]))
            if waits:
                self.prog[n].append((waits, None, None))
                self.ninst += len(waits)

    def emit(self):
        nc = self.nc
        with nc.Block() as block:
            for n in self.names:
                prog = self.prog[n]
                if not prog:
                    continue

                def body(e, prog=prog):
                    for waits, fn, inc in prog:
                        for k, v in waits:
                            e.wait_ge(self._semobj(k), v)
                        if fn is not None:
                            fn(e).then_inc(self._semobj(inc[0]), inc[1])
                getattr(block, n)(body)
        self.prog = {n: [] for n in self.names}


def vec_index(DEPTH):
    n_a = (DEPTH + 1) // 2
    idx = {}
    names = []
    for i in range(DEPTH):
        names += ["n1_%d" % i, "n2_%d" % i]
    names.append("fin")
    for a in range(n_a):
        names += ["mu%d_%d" % (j, a) for j in range(6)]
        names += ["w0_%d_%d" % (d, a) for d in range(2)] + ["a0_%d_%d" % (d, a) for d in range(2)]
        names += ["kk_%d" % a, "ka_%d" % a, "rk_%d" % a, "lw_%d" % a, "lb_%d" % a]
        if a >= 1:
            names.append("v0_%d" % a)
    for k, n in enumerate(names):
        idx[n] = k
    return idx


def build(SEQ, CTX, DEPTH, dbg=False, phases=None):
    assert CTX == 256 and SEQ % 512 == 0
    nc = bass.Bass("TRN2", target_bir_lowering=False)
    TT = CTX + SEQ
    NCH = TT // 128
    NCC = CTX // 128
    last_a = ((DEPTH - 1) // 2) * 2
    n_a = (DEPTH + 1) // 2
    n_b = DEPTH // 2
    n_v = max(n_a - 1, 1)
    vidx = vec_index(DEPTH)
    NV = len(vidx)

    def din(name, shape, dt=F32):
        return nc.dram_tensor(name, list(shape), dt, kind="ExternalInput").ap()

    def dscr(name, shape, dt=F32):
        return nc.dram_tensor(name, list(shape), dt, kind="ExternalOutput" if dbg else "Internal").ap()

    xin = din("xin", [SEQ, D]); cin = din("cin", [CTX, D]); ccin = din("cc", [128, 16])
    modw = din("modw", [DEPTH, D, NMOD * D]); modb = din("modb", [128, DEPTH * 48])
    vecs_in = din("vecs", [128, NV * 8])
    w1m = din("w1m", [DEPTH, D, 4 * D]); w2m = din("w2m", [DEPTH, 4 * D, D])
    wr_in = din("wr", [n_a, D, D]); wk_in = din("wk", [n_a, D, D]); wv_in = din("wv", [n_a, D, D]); wo_in = din("wo", [n_a, D, D])
    lw1_in = din("lw1", [n_a, D, 128]); lw2_in = din("lw2", [n_a, 128, D])
    la1_in = din("la1", [n_a, D, 128]); la2_in = din("la2", [n_a, 128, D])
    lg1_in = din("lg1", [n_a, D, 128]); lg2_in = din("lg2", [n_a, 128, D])
    lv1_in = din("lv1", [n_v, D, 32]); lv2_in = din("lv2", [n_v, 32, D])
    fwo_in = din("fwo", [max(n_b, 1), D, D])
    idb_in = din("idb", [128, 128], BF16); idf_in = din("idf", [128, 128]); ones_in = din("onesf", [128, 128]); blk_in = din("blk", [128, 128])
    mam_in = din("mam", [2, 128, 512], BF16); mn_in = din("mn", [2, 128, 512], BF16)
    dftc_in = din("dftc", [128, 2, 512], BF16)
    dll_in = din("dll", [2, SEQ, SEQ], BF16); dlc_in = din("dlc", [2, CTX, CTX], BF16)
    out = nc.dram_tensor("out", [SEQ, D], F32, kind="ExternalOutput").ap()

    X = dscr("X", [D, TT])
    OPA = [dscr("OPA%d" % d, [NCH, 128, 8, 256], BF16) for d in range(2)]
    OPB = [dscr("OPB%d" % d, [NCH, 128, 8, 128], BF16) for d in range(2)]
    OPK = [dscr("OPK%d" % d, [NCH, 128, 8, 128], BF16) for d in range(2)]
    TMB = [dscr("TMB%d" % d, [NCH, 128, D], BF16) for d in range(2)]
    TMK = [dscr("TMK%d" % d, [NCH, 128, D], BF16) for d in range(2)]
    TMV = dscr("TMV", [NCH, 128, D], BF16)
    PCD = [dscr("PCD%d" % d, [NCH, 128, 8]) for d in range(2)]
    G = dscr("G", [D, TT], BF16); BN = dscr("BN", [D, TT], BF16); VF = dscr("VF", [D, TT])
    Y = [dscr("Y%d" % d, [NCH, 128, D]) for d in range(2)]
    YCS = dscr("YCS", [NCH, 128, 2, D], BF16)

    def cb():
        return [Buf() for _ in range(NCH)]
    bX = cb(); bOP = [cb(), cb()]; bG = cb(); bVF = cb(); bY = [cb(), cb()]; bYCS = cb()

    def fm(ap, t0, w):
        return ap[:, t0:t0 + w].rearrange("(c p) t -> p c t", p=128)

    with ExitStack() as gs:
        S = Sched(nc, gs)
        uid = [0]

        def sbt(stk, shape, dt=F32):
            uid[0] += 1
            return Buf(stk.enter_context(nc.sbuf_tensor("t%d" % uid[0], list(shape), dt)))

        PS = [Buf(gs.enter_context(nc.psum_tensor("ps%d" % i, [128, 512], F32))) for i in range(8)]
        psn = [0]

        def nextps():
            b = PS[psn[0] % 8]
            psn[0] += 1
            return b

        def mm(o, lhsT, rhs, start, stop, R, Wr):
            S.op("tensor", lambda e: e.matmul(o, lhsT=lhsT, rhs=rhs, start=start, stop=stop), R, Wr)

        def tr(o, i, ident, R, Wr):
            S.op("tensor", lambda e: e.transpose(out=o, in_=i, identity=ident), R, Wr)

        def tt(eng, o, a, b, op, R, Wr):
            S.op(eng, lambda e: e.tensor_tensor(out=o, in0=a, in1=b, op=op), R, Wr)

        def ts(eng, o, a, s1, op0, R, Wr, s2=None, op1=None):
            if op1 is None:
                S.op(eng, lambda e: e.tensor_scalar(out=o, in0=a, scalar1=s1, scalar2=None, op0=op0), R, Wr)
            else:
                S.op(eng, lambda e: e.tensor_scalar(out=o, in0=a, scalar1=s1, scalar2=s2, op0=op0, op1=op1), R, Wr)

        def stt(eng, o, a, s, b, op0, op1, R, Wr):
            S.op(eng, lambda e: e.scalar_tensor_tensor(out=o, in0=a, scalar=s, in1=b, op0=op0, op1=op1), R, Wr)

        def act(o, i, func, R, Wr, bias=None, scale=1.0):
            if bias is None:
                S.op("scalar", lambda e: e.activation(out=o, in_=i, func=func, scale=scale), R, Wr)
            else:
                S.op("scalar", lambda e: e.activation(out=o, in_=i, func=func, bias=bias, scale=scale), R, Wr)

        def cp(eng, o, i, R, Wr):
            if eng == "scalar":
                S.op(eng, lambda e: e.copy(out=o, in_=i), R, Wr)
            else:
                S.op(eng, lambda e: e.tensor_copy(out=o, in_=i), R, Wr)

        def rsum(eng, o, i, R, Wr):
            S.op(eng, lambda e: e.reduce_sum(out=o, in_=i, axis=AX.X), R, Wr)

        def recip(o, i, R, Wr):
            S.op("vector", lambda e: e.reciprocal(out=o, in_=i), R, Wr)

        def mset(eng, o, val, Wr):
            S.op(eng, lambda e: e.memset(o, val), (), Wr)

        def scan(o, d0, d1, R, Wr):
            S.op("vector", lambda e: e.tensor_tensor_scan(out=o, data0=d0, data1=d1, initial=0.0, op0=ALU.mult, op1=ALU.add), R, Wr)

        LD, ST = "sync", "gpsimd"

        def phase_end():
            S.barrier()
            S.emit()

        idb = sbt(gs, [128, 128], BF16); idf = sbt(gs, [128, 128]); onesf = sbt(gs, [128, 128]); blk = sbt(gs, [128, 128])
        mam = sbt(gs, [128, 2, 512], BF16); mn = sbt(gs, [128, 2, 512], BF16)
        vecs = sbt(gs, [128, NV, 8]); cc = sbt(gs, [128, 8, 2]); sc = sbt(gs, [128, 8, 2])
        modT = sbt(gs, [128, DEPTH * 2 * 48]); kst = sbt(gs, [128, 4])
        lay = sbt(gs, [128, 2, 6, 8])
        S.dma(LD, idb[:], idb_in, (), [idb]); S.dma(LD, idf[:], idf_in, (), [idf])
        S.dma(LD, onesf[:], ones_in, (), [onesf]); S.dma(LD, blk[:], blk_in, (), [blk])
        S.dma(LD, mam[:], mam_in.rearrange("d p n -> p d n"), (), [mam]); S.dma(LD, mn[:], mn_in.rearrange("d p n -> p d n"), (), [mn])
        S.dma(LD, vecs[:], vecs_in.rearrange("p (v c) -> p v c", c=8), (), [vecs])
        S.dma(LD, cc[:], ccin.rearrange("p (c w) -> p c w", w=2), (), [cc])
        mset("vector", kst[:, 0:1], 1e-6, [kst]); mset("vector", kst[:, 1:2], 64e-5, [kst]); mset("vector", kst[:, 2:3], 0.0, [kst])
        act(sc[:], cc[:], AF.Silu, [cc], [sc])

        def vcol(name):
            return vecs[:, vidx[name], :]

        def bc_t(col, w):
            return col.unsqueeze(2).broadcast_to([128, 8, w])

        with ExitStack() as ph:
            wt = [sbt(ph, [128, 8, 512]) for _ in range(2)]
            mbT = sbt(ph, [128, DEPTH, 48])
            S.dma(LD, mbT[:], modb.rearrange("p (i j) -> p i j", i=DEPTH), (), [mbT])
            modT4 = modT[:].rearrange("p (i w j) -> p i w j", i=DEPTH, w=2)
            import os
            for i in range(DEPTH if not os.environ.get('K_SKIPMOD') else 0):
                for n in range(12):
                    w = wt[n % 2]
                    S.dma(LD, w[:], modw[i, :, n * 512:(n + 1) * 512].rearrange("(c p) n -> p c n", p=128), (), [w])
                    pb = nextps()
                    for q in range(4):
                        for kc in range(8):
                            mm(pb[:, q * 2:(q + 1) * 2], w[:, kc, q * 128:(q + 1) * 128], sc[:, kc, :], kc == 0, kc == 7, [sc, w], [pb])
                    cp("vector", modT4[:, i, :, n * 4:(n + 1) * 4], pb[:, 0:8].rearrange("p (q w) -> p w q", w=2), [pb], [modT])
                for w_ in range(2):
                    tt("vector", modT4[:, i, w_, :], modT4[:, i, w_, :], mbT[:, i, :], ALU.add, [modT, mbT], [modT])
            xt_ = [sbt(ph, [128, D]) for _ in range(2)]
            xo_ = [sbt(ph, [128, 8, 128]) for _ in range(2)]
            for ci in range(NCH):
                xt = xt_[ci % 2]; xo = xo_[ci % 2]
                src = cin[ci * 128:(ci + 1) * 128, :] if ci < NCC else xin[(ci - NCC) * 128:(ci - NCC + 1) * 128, :]
                S.dma(LD, xt[:], src, (), [xt])
                for hf in range(2):
                    pb = nextps()
                    for q in range(4):
                        c = hf * 4 + q
                        tr(pb[:, q * 128:(q + 1) * 128], xt[:, c * 128:(c + 1) * 128], idf[:], [xt, idf], [pb])
                    cp("vector" if hf == 0 else "scalar", xo[:, hf * 4:(hf + 1) * 4, :], pb[:].rearrange("p (q t) -> p q t", q=4), [pb], [xo])
                S.dma(ST, fm(X, ci * 128, 128), xo[:], [xo], [bX[ci]])
            phase_end()

        def set_layer(i):
            for kind in range(2):
                base = (i * 2 + kind) * 48
                m = lambda j: modT[:, base + j * 8: base + (j + 1) * 8]
                for half, gname in ((0, "n1_%d" % i), (1, "n2_%d" % i)):
                    stt("vector", lay[:, kind, half * 3 + 0, :], m(half * 3 + 1), 1.0, vcol(gname), ALU.add, ALU.mult, [modT, vecs], [lay])
                    cp("vector", lay[:, kind, half * 3 + 1, :], m(half * 3 + 0), [modT], [lay])
                    cp("vector", lay[:, kind, half * 3 + 2, :], m(half * 3 + 2), [modT], [lay])

        def load_w(stk_tiles, dst, src_ap, rows_per, ncols):
            stg = stk_tiles
            KC = src_ap.shape[0] // 128
            step = max(1, 4096 // ncols)
            k = 0
            i = 0
            while k < KC:
                n = min(step, KC - k)
                s = stg[i % 2]
                sv = s[:, 0:n * ncols].rearrange("p (c n) -> p c n", c=n)
                S.dma(LD, sv, src_ap[k * 128:(k + n) * 128, :].rearrange("(c p) n -> p c n", p=128), (), [s])
                cp("gpsimd" if i % 2 == 0 else "vector", dst[:, k:k + n, :], sv, [s], [dst])
                k += n
                i += 1

        def load_small(stg, dst_ap, dstbuf, src_ap, rows, ncols):
            s = stg[0]
            S.dma(LD, s[0:rows, 0:ncols], src_ap, (), [s])
            cp("vector", dst_ap, s[0:rows, 0:ncols], [s], [dstbuf])

        def rms_rstd(x, w, sq_, rstd):
            pb = nextps()
            for c in range(8):
                sq = sq_[c % 2]
                act(sq[:, 0:w], x[:, c, 0:w], AF.Square, [x], [sq])
                mm(pb[:, 0:w], onesf[:], sq[:, 0:w], c == 0, c == 7, [onesf, sq], [pb])
            act(rstd[:, 0:w], pb[:, 0:w], AF.Sqrt, [pb, kst], [rstd], bias=kst[:, 0:1], scale=1.0 / D)
            recip(rstd[:, 0:w], rstd[:, 0:w], [rstd], [rstd])

        def norm_mod(x, w, kind, half, sq_, rstd, dst, dst_off=0):
            rms_rstd(x, w, sq_, rstd)
            tt("vector", x[:, :, 0:w], x[:, :, 0:w], rstd[:, 0:w].unsqueeze(1).broadcast_to([128, 8, w]), ALU.mult, [x, rstd], [x])
            for c in range(8):
                act(dst[:, c, dst_off:dst_off + w], x[:, c, 0:w], AF.Identity, [x, lay], [dst],
                    bias=lay[:, kind, half * 3 + 1, c:c + 1], scale=lay[:, kind, half * 3 + 0, c:c + 1])

        def chunk_kind(ci):
            return 1 if ci < NCC else 0

        def mlp_phase(i, do_ctx):
            with ExitStack() as ph:
                w1 = sbt(ph, [128, 8, 4 * D], BF16); w2 = sbt(ph, [128, 32, D], BF16)
                with ExitStack() as sp:
                    stg = [sbt(sp, [128, 4096]) for _ in range(2)]
                    load_w(stg, w1, w1m[i], 128, 4 * D)
                    load_w(stg, w2, w2m[i], 128, D)
                    phase_end()
                W = 256
                xt_ = [sbt(ph, [128, 8, W]) for _ in range(2)]
                xn = sbt(ph, [128, 8, W]); hb = sbt(ph, [128, 8, W], BF16); hid = sbt(ph, [128, 32, W], BF16)
                tmp_ = [sbt(ph, [128, 512]) for _ in range(2)]
                sq_ = [sbt(ph, [128, W]) for _ in range(2)]; rstd = sbt(ph, [128, W])
                tiles = list(range(0 if do_ctx else NCC, NCH, 2))
                for n, c0 in enumerate(tiles):
                    kind = chunk_kind(c0)
                    xt = xt_[n % 2]
                    S.dma(LD, xt[:], fm(X, c0 * 128, W), [bX[c0], bX[c0 + 1]], [xt])
                    cp("gpsimd", xn[:], xt[:], [xt], [xn])
                    norm_mod(xn, W, kind, 1, sq_, rstd, hb)
                    for mg in range(16):
                        pb = nextps()
                        for q in range(2):
                            m = mg * 2 + q
                            for kc in range(8):
                                mm(pb[:, q * W:(q + 1) * W], w1[:, kc, m * 128:(m + 1) * 128], hb[:, kc, :], kc == 0, kc == 7, [w1, hb], [pb])
                        tp = tmp_[mg % 2]
                        act(tp[:], pb[:], AF.Relu, [pb], [tp])
                        tt("vector" if mg % 2 == 0 else "gpsimd", hid[:, mg * 2:mg * 2 + 2, :], tp[:].rearrange("p (q t) -> p q t", q=2),
                           tp[:].rearrange("p (q t) -> p q t", q=2), ALU.mult, [tp], [hid])
                    for mg in range(4):
                        pb = nextps()
                        for q in range(2):
                            m = mg * 2 + q
                            for kc in range(32):
                                mm(pb[:, q * W:(q + 1) * W], w2[:, kc, m * 128:(m + 1) * 128], hid[:, kc, :], kc == 0, kc == 31, [w2, hid], [pb])
                        for q in range(2):
                            m = mg * 2 + q
                            stt("vector", xt[:, m, :], pb[:, q * W:(q + 1) * W], lay[:, kind, 5, m:m + 1], xt[:, m, :], ALU.mult, ALU.add, [pb, lay, xt], [xt])
                    S.dma(ST, fm(X, c0 * 128, W), xt[:], [xt], [bX[c0], bX[c0 + 1]])
                phase_end()

        def rwkv_p1(i, idx):
            with ExitStack() as ph:
                wr = sbt(ph, [128, 8, D], BF16); wk = sbt(ph, [128, 8, D], BF16); wv = sbt(ph, [128, 8, D], BF16)
                lw1 = sbt(ph, [128, 8, 128], BF16); la1 = sbt(ph, [128, 8, 128], BF16); lg1 = sbt(ph, [128, 8, 128], BF16); lv1 = sbt(ph, [128, 8, 32], BF16)
                lw2 = sbt(ph, [128, D], BF16); la2 = sbt(ph, [128, D], BF16); lg2 = sbt(ph, [128, D], BF16); lv2 = sbt(ph, [32, D], BF16)
                with ExitStack() as sp:
                    stg = [sbt(sp, [128, 4096]) for _ in range(2)]
                    load_w(stg, wr, wr_in[idx], 128, D); load_w(stg, wk, wk_in[idx], 128, D); load_w(stg, wv, wv_in[idx], 128, D)
                    load_w(stg, lw1, lw1_in[idx], 128, 128); load_w(stg, la1, la1_in[idx], 128, 128); load_w(stg, lg1, lg1_in[idx], 128, 128)
                    load_small(stg, lw2[:], lw2, lw2_in[idx], 128, D); load_small(stg, la2[:], la2, la2_in[idx], 128, D)
                    load_small(stg, lg2[:], lg2, lg2_in[idx], 128, D)
                    if idx >= 1:
                        load_w(stg, lv1, lv1_in[idx - 1], 128, 32)
                        load_small(stg, lv2[:], lv2, lv2_in[idx - 1], 32, D)
                    phase_end()
                WH = 256
                xh = sbt(ph, [128, 8, WH]); sq_ = [sbt(ph, [128, WH]) for _ in range(2)]; rstd = sbt(ph, [128, WH])
                Sx = [sbt(ph, [128, 8, 128]) for _ in range(7)]
                r_t = sbt(ph, [128, 8, 128]); k_t = sbt(ph, [128, 8, 128]); v_t = sbt(ph, [128, 8, 128]); kkn = sbt(ph, [128, 8, 128])
                mix = [sbt(ph, [128, 8, 128], BF16) for _ in range(6)]
                lb_ = [sbt(ph, [128, 128], BF16) for _ in range(4)]
                ar = [sbt(ph, [128, 8, 256], BF16) for _ in range(2)]
                bt = [sbt(ph, [128, 8, 128], BF16) for _ in range(2)]
                kt = [sbt(ph, [128, 8, 128], BF16) for _ in range(2)]
                tms = [sbt(ph, [128, D], BF16) for _ in range(2)]
                gb = sbt(ph, [128, 8, 128], BF16); bnb = sbt(ph, [128, 8, 128], BF16); vb = sbt(ph, [128, 8, 128], BF16)
                pct = [sbt(ph, [128, 8]) for _ in range(2)]
                omka = sbt(ph, [128, 8]); tott = sbt(ph, [128, 8])
                ts("vector", omka[:], vcol("ka_%d" % idx), -1.0, ALU.mult, [vecs], [omka], 1.0, ALU.add)
                tmc = [0]

                def proj_big(w, mx, dst, dst_eng):
                    for mg in range(2):
                        pb = nextps()
                        for q in range(4):
                            m = mg * 4 + q
                            for kc in range(8):
                                mm(pb[:, q * 128:(q + 1) * 128], w[:, kc, m * 128:(m + 1) * 128], mx[:, kc, :], kc == 0, kc == 7, [w, mx], [pb])
                        cp(dst_eng, dst[:, mg * 4:(mg + 1) * 4, :], pb[:].rearrange("p (q t) -> p q t", q=4), [pb], [dst])

                def lora1(w, mx, ncol, func, dstb):
                    pb = nextps()
                    for kc in range(8):
                        mm(pb[0:ncol, 0:128], w[:, kc, 0:ncol], mx[:, kc, :], kc == 0, kc == 7, [w, mx], [pb])
                    act(dstb[0:ncol, :], pb[0:ncol, 0:128], func, [pb], [dstb])

                def lora2(w2, lo, hi, lbuf, bias_name, dst):
                    for mg in range(2):
                        pb = nextps()
                        for q in range(4):
                            m = mg * 4 + q
                            mm(pb[:, q * 128:(q + 1) * 128], w2[lo:hi, m * 128:(m + 1) * 128], lbuf[lo:hi, :], True, True, [w2, lbuf], [pb])
                        for q in range(4):
                            m = mg * 4 + q
                            if bias_name is None:
                                cp("vector", dst[:, m, :], pb[:, q * 128:(q + 1) * 128], [pb], [dst])
                            else:
                                act(dst[:, m, :], pb[:, q * 128:(q + 1) * 128], AF.Sigmoid, [pb, vecs], [dst], bias=vecs[:, vidx[bias_name], m:m + 1])

                def headsum(src, dstfn):
                    for mg in range(2):
                        pb = nextps()
                        for q in range(4):
                            c = mg * 4 + q
                            mm(pb[:, q * 128:(q + 1) * 128], blk[:], src[:, c, :], True, True, [blk, src], [pb])
                        dstfn(mg, pb)

                def to_tm(src, dst_dram, dbuf):
                    pb = nextps()
                    pbb = pb[:].bitcast(BF16)
                    for c in range(8):
                        tr(pbb[:, c * 128:(c + 1) * 128], src[:, c, :], idb[:], [src, idb], [pb])
                    t = tms[tmc[0] % 2]
                    tmc[0] += 1
                    cp("scalar", t[:], pbb, [pb], [t])
                    S.dma(ST, dst_dram, t[:], [t], [dbuf])

                for ci in range(NCH):
                    kind = chunk_kind(ci)
                    t0 = ci * 128
                    lo_lim, hi_lim = (0, CTX) if kind == 1 else (CTX, TT)
                    lo = max(t0 - 64, lo_lim); hi = min(t0 + 192, hi_lim)
                    o0 = lo - (t0 - 64); o1 = hi - (t0 - 64)
                    if o0 > 0:
                        mset("gpsimd", xh[:, :, 0:o0], 0.0, [xh])
                    if o1 < WH:
                        mset("gpsimd", xh[:, :, o1:WH], 0.0, [xh])
                    S.dma(LD, xh[:, :, o0:o1], fm(X, lo, hi - lo), [bX[c] for c in range(lo // 128, (hi - 1) // 128 + 1)], [xh])
                    norm_mod(xh, WH, kind, 0, sq_, rstd, xh)
                    if o0 > 0:
                        mset("gpsimd", xh[:, :, 0:o0], 0.0, [xh])
                    if o1 < WH:
                        mset("gpsimd", xh[:, :, o1:WH], 0.0, [xh])
                    xx = Sx[0]
                    hc = xh[:, :, 64:192]
                    if kind == 0:
                        tt("vector", xx[:, 0:2, :], xh[:, 0:2, 63:191], xh[:, 0:2, 64:192], ALU.subtract, [xh], [xx])
                        tt("vector", xx[:, 2:4, :], xh[:, 2:4, 65:193], xh[:, 2:4, 64:192], ALU.subtract, [xh], [xx])
                        tt("gpsimd", xx[:, 4:6, :], xh[:, 4:6, 0:128], xh[:, 4:6, 64:192], ALU.subtract, [xh], [xx])
                        tt("gpsimd", xx[:, 6:8, :], xh[:, 6:8, 128:256], xh[:, 6:8, 64:192], ALU.subtract, [xh], [xx])
                        for j in (0, 64):
                            ts("vector", xx[:, 0:2, j:j + 1], xh[:, 0:2, 64 + j:65 + j], -1.0, ALU.mult, [xh, xx], [xx])
                            ts("vector", xx[:, 2:4, j + 63:j + 64], xh[:, 2:4, 127 + j:128 + j], -1.0, ALU.mult, [xh, xx], [xx])
                    else:
                        tt("vector", xx[:, 0:4, :], xh[:, 0:4, 63:191], xh[:, 0:4, 64:192], ALU.subtract, [xh], [xx])
                        tt("gpsimd", xx[:, 4:8, :], xh[:, 4:8, 65:193], xh[:, 4:8, 64:192], ALU.subtract, [xh], [xx])
                    for j in range(6):
                        eng = "vector" if j % 2 == 0 else "gpsimd"
                        tmp = Sx[1 + j % 2]
                        tt(eng, tmp[:], xx[:], bc_t(vcol("mu%d_%d" % (j, idx)), 128), ALU.mult, [xx, vecs], [tmp])
                        tt(eng, mix[j][:], tmp[:], hc, ALU.add, [tmp, xh], [mix[j]])
                    proj_big(wr, mix[0], r_t, "vector"); proj_big(wk, mix[2], k_t, "scalar"); proj_big(wv, mix[3], v_t, "vector")
                    wlb, alb, glb, vlb = lb_
                    lora1(lw1, mix[1], 128, AF.Tanh, wlb)
                    lora1(la1, mix[4], 128, AF.Identity, alb)
                    lora1(lg1, mix[5], 128, AF.Sigmoid, glb)
                    lora2(lg2, 0, 128, glb, None, Sx[1])
                    cp("gpsimd", gb[:], Sx[1][:], [Sx[1]], [gb])
                    S.dma(ST, fm(G, t0, 128), gb[:], [gb], [bG[ci]])
                    if idx >= 1:
                        lora1(lv1, mix[3], 32, AF.Identity, vlb)
                        sv = Sx[1]; vf = Sx[2]
                        lora2(lv2, 0, 32, vlb, "v0_%d" % idx, sv)
                        S.dma(LD, vf[:], fm(VF, t0, 128), [bVF[ci]], [vf])
                        tt("vector", vf[:], vf[:], v_t[:], ALU.subtract, [vf, v_t], [vf])
                        tt("vector", vf[:], vf[:], sv[:], ALU.mult, [vf, sv], [vf])
                        tt("vector", v_t[:], v_t[:], vf[:], ALU.add, [vf, v_t], [v_t])
                    else:
                        S.dma(ST, fm(VF, t0, 128), v_t[:], [v_t], [bVF[ci]])
                    cp("gpsimd", vb[:], v_t[:], [v_t], [vb])
                    to_tm(vb, TMV[ci], bOP[0][ci])
                    kkw = Sx[1]; sqk = Sx[2]
                    tt("vector", kkw[:], k_t[:], bc_t(vcol("kk_%d" % idx), 128), ALU.mult, [k_t, vecs], [kkw])
                    act(sqk[:], kkw[:], AF.Square, [kkw], [sqk])
                    rn = Sx[3]

                    def kkfin(mg, pb):
                        act(rn[:, mg * 4:(mg + 1) * 4, :], pb[:].rearrange("p (q t) -> p q t", q=4), AF.Sqrt, [pb], [rn])
                    headsum(sqk, kkfin)
                    ts("vector", rn[:], rn[:], 1e-12, ALU.max, [rn], [rn])
                    recip(rn[:], rn[:], [rn], [rn])
                    tt("vector", kkn[:], kkw[:], rn[:], ALU.mult, [kkw, rn], [kkn])
                    kdsum = Sx[6]
                    for d in range(2):
                        sg = Sx[1]; cs = Sx[2]; icl = Sx[3]; s4 = Sx[4]
                        lora2(lw2, d * 64, d * 64 + 64, wlb, "w0_%d_%d" % (d, idx), sg)
                        lora2(la2, d * 64, d * 64 + 64, alb, "a0_%d_%d" % (d, idx), icl)
                        for c in range(8):
                            scan(cs[:, c, :], onesf[:, 0:128], sg[:, c, :], [onesf, sg], [cs])
                        cp("vector", tott[:].unsqueeze(2), cs[:, :, 127:128], [cs], [tott])
                        tot = tott[:].unsqueeze(2)
                        act(pct[d][:], tott[:], AF.Exp, [tott], [pct[d]], scale=-C0)
                        S.dma(ST, PCD[d][ci], pct[d][:], [pct[d]], [bOP[d][ci]])
                        if d == 0:
                            tt("vector", sg[:], cs[:], sg[:], ALU.subtract, [cs, sg], [sg])
                            e1, e2 = cs, sg
                        else:
                            stt("vector", cs[:], cs[:], -1.0, tot.broadcast_to([128, 8, 128]), ALU.mult, ALU.add, [cs, tott], [cs])
                            tt("vector", sg[:], cs[:], sg[:], ALU.add, [cs, sg], [sg])
                            e1, e2 = sg, cs
                        a = ar[d]
                        act(e2[:], e2[:], AF.Exp, [e2], [e2], scale=-C0)
                        stt("vector", a[:, :, 0:128], kkn[:], -1.0, e2[:], ALU.mult, ALU.mult, [kkn, e2], [a])
                        act(s4[:], e1[:], AF.Exp, [e1], [s4], scale=-C0)
                        tt("vector", a[:, :, 128:256], r_t[:], s4[:], ALU.mult, [r_t, s4], [a])
                        act(e1[:], e1[:], AF.Exp, [e1], [e1], scale=C0)
                        for c in range(8):
                            act(s4[:, c, :], icl[:, c, :], AF.Identity, [icl, vecs, omka], [s4], bias=omka[:, c:c + 1], scale=vecs[:, vidx["ka_%d" % idx], c:c + 1])
                        tt("vector", s4[:], s4[:], k_t[:], ALU.mult, [s4, k_t], [s4])
                        if d == 0:
                            cp("gpsimd", kdsum[:], s4[:], [s4], [kdsum])
                        else:
                            tt("gpsimd", kdsum[:], kdsum[:], s4[:], ALU.add, [s4, kdsum], [kdsum])
                        tt("vector", kt[d][:], s4[:], e1[:], ALU.mult, [s4, e1], [kt[d]])
                        tt("gpsimd", icl[:], icl[:], kkn[:], ALU.mult, [icl, kkn], [icl])
                        tt("gpsimd", bt[d][:], icl[:], e1[:], ALU.mult, [icl, e1], [bt[d]])
                        S.dma(ST, OPA[d][ci], a[:], [a], [bOP[d][ci]])
                        S.dma(ST, OPB[d][ci], bt[d][:], [bt[d]], [bOP[d][ci]])
                        S.dma(ST, OPK[d][ci], kt[d][:], [kt[d]], [bOP[d][ci]])
                        to_tm(bt[d], TMB[d][ci], bOP[d][ci])
                        to_tm(kt[d], TMK[d][ci], bOP[d][ci])
                    bs = Sx[1]
                    tt("vector", bs[:], r_t[:], kdsum[:], ALU.mult, [r_t, kdsum], [bs])
                    tt("vector", bs[:], bs[:], bc_t(vcol("rk_%d" % idx), 128), ALU.mult, [bs, vecs], [bs])

                    def bfin(mg, pb):
                        tt("vector", bnb[:, mg * 4:(mg + 1) * 4, :], pb[:].rearrange("p (q t) -> p q t", q=4), v_t[:, mg * 4:(mg + 1) * 4, :], ALU.mult, [pb, v_t], [bnb])
                    headsum(bs, bfin)
                    S.dma(ST, fm(BN, t0, 128), bnb[:], [bnb], [bG[ci]])
                phase_end()

        def rwkv_p2(ctx_readout):
            with ExitStack() as ph:
                are_ = [sbt(ph, [128, 8, 256], BF16) for _ in range(2)]
                aro_ = [sbt(ph, [128, 8, 256], BF16) for _ in range(2)]
                for _t in are_:
                    mset("vector", _t[64:128, :, :], 0.0, [_t])
                for _t in aro_:
                    mset("vector", _t[0:64, :, :], 0.0, [_t])
                bt_ = [sbt(ph, [128, 8, 128], BF16) for _ in range(2)]
                kt_ = [sbt(ph, [128, 8, 128], BF16) for _ in range(2)]
                tmb_ = [sbt(ph, [128, D], BF16) for _ in range(2)]
                tmk_ = [sbt(ph, [128, D], BF16) for _ in range(2)]
                tmv_ = [sbt(ph, [128, D], BF16) for _ in range(2)]
                pc_ = [sbt(ph, [128, 8]) for _ in range(2)]
                ABM = sbt(ph, [128, 16, 256], BF16); AKM = sbt(ph, [128, 16, 256], BF16)
                Mb = [sbt(ph, [128, 16, 128], BF16) for _ in range(2)]
                Mtb = [sbt(ph, [128, 16, 128], BF16) for _ in range(2)]
                Ttb = [sbt(ph, [128, 16, 128], BF16) for _ in range(2)]
                X0b = sbt(ph, [128, 16, 64], BF16); Ub = sbt(ph, [128, 16, 64], BF16)
                ysb_ = [sbt(ph, [128, D]) for _ in range(2)]
                Hf = sbt(ph, [128, 8, 64]); Hb = sbt(ph, [128, 8, 64], BF16); Htmp = sbt(ph, [128, 8, 64])
                n = 0
                for d in range(2):
                    order = list(range(NCH)) if d == 0 else ([1, 0] + list(range(NCH - 1, NCC - 1, -1)))
                    mset("vector", Hf[:], 0.0, [Hf]); mset("gpsimd", Hb[:], 0.0, [Hb])
                    for ci in order:
                        need_y = (ci >= NCC) or ctx_readout
                        are = are_[n % 2]; aro = aro_[n % 2]; arz = (are, aro); bt = bt_[n % 2]; kt = kt_[n % 2]; tmb = tmb_[n % 2]; tmk = tmk_[n % 2]; tmv = tmv_[n % 2]; pc = pc_[n % 2]
                        ysb = ysb_[n % 2]
                        n += 1
                        dep = [bOP[d][ci], bOP[0][ci]]
                        S.dma(LD, are[0:64, :, :], OPA[d][ci][0:64], dep, [are]); S.dma(LD, aro[64:128, :, :], OPA[d][ci][64:128], dep, [aro]); S.dma(LD, bt[:], OPB[d][ci], dep, [bt]); S.dma(LD, kt[:], OPK[d][ci], dep, [kt])
                        S.dma(LD, tmb[:], TMB[d][ci], dep, [tmb]); S.dma(LD, tmk[:], TMK[d][ci], dep, [tmk]); S.dma(LD, tmv[:], TMV[ci], dep, [tmv])
                        S.dma(LD, pc[:], PCD[d][ci], dep, [pc])
                        for hp in range(8):
                            for (lt, dstm) in ((bt, ABM), (kt, AKM)):
                                pb = nextps()
                                for e_ in range(2):
                                    mm(pb[:, e_ * 256:(e_ + 1) * 256], lt[:, hp, :], arz[e_][:, hp, :], True, True, [lt, arz[e_]], [pb])
                                tt("vector", dstm[:, hp * 2:hp * 2 + 2, :], pb[:].rearrange("p (h n) -> p h n", h=2), mam[:, d, :].rearrange("p (h n) -> p h n", h=2), ALU.mult, [pb, mam], [dstm])
                        for hq in range(4):
                            pb = nextps()
                            for q in range(4):
                                h = hq * 4 + q
                                c = h // 2; po = (h % 2) * 64
                                mm(pb[:, q * 128:(q + 1) * 128], arz[h % 2][:, c, 0:128], bt[:, c, :], True, True, [arz[h % 2], bt], [pb])
                            tt("vector", Mb[0][:, hq * 4:hq * 4 + 4, :], pb[:].rearrange("p (h n) -> p h n", h=4), mn[:, d, :].rearrange("p (h n) -> p h n", h=4), ALU.mult, [pb, mn], [Mb[0]])
                        import os
                        lvl = int(os.environ.get('K_P2', '9'))
                        if lvl < 2:
                            continue
                        tt("gpsimd", Ttb[0][:], ABM[:, :, 0:128], idb[:].unsqueeze(1).broadcast_to([128, 16, 128]), ALU.add, [ABM, idb], [Ttb[0]])

                        def Mt_of(j, h):
                            return (ABM[:, h, 0:128], ABM) if j == 0 else (Mtb[j % 2][:, h, :], Mtb[j % 2])
                        for j in range(1, 8):
                            for hq in range(4):
                                hs = range(hq * 4, hq * 4 + 4)
                                if j <= 6:
                                    pb = nextps()
                                    for q, h in enumerate(hs):
                                        mt, mtb = Mt_of(j - 1, h)
                                        mm(pb[:, q * 128:(q + 1) * 128], mt, Mb[(j - 1) % 2][:, h, :], True, True, [mtb, Mb[(j - 1) % 2]], [pb])
                                    cp("scalar", Mb[j % 2][:, hq * 4:hq * 4 + 4, :], pb[:].rearrange("p (h n) -> p h n", h=4), [pb], [Mb[j % 2]])
                                if j <= 5:
                                    pb = nextps()
                                    for q, h in enumerate(hs):
                                        mt, mtb = Mt_of(j - 1, h)
                                        mm(pb[:, q * 128:(q + 1) * 128], Mb[(j - 1) % 2][:, h, :], mt, True, True, [mtb, Mb[(j - 1) % 2]], [pb])
                                    cp("scalar", Mtb[j % 2][:, hq * 4:hq * 4 + 4, :], pb[:].rearrange("p (h n) -> p h n", h=4), [pb], [Mtb[j % 2]])
                                if j >= 2:
                                    pb = nextps()
                                    src = Ttb[(j - 2) % 2]; dst = Ttb[(j - 1) % 2]
                                    for q, h in enumerate(hs):
                                        mm(pb[:, q * 128:(q + 1) * 128], Mb[(j - 1) % 2][:, h, :], src[:, h, :], True, True, [Mb[(j - 1) % 2], src], [pb])
                                    tt("vector", dst[:, hq * 4:hq * 4 + 4, :], pb[:].rearrange("p (h n) -> p h n", h=4), src[:, hq * 4:hq * 4 + 4, :], ALU.add, [pb, src], [dst])
                        Tt = Ttb[0]
                        if lvl < 3:
                            continue
                        for hg in range(2):
                            pb = nextps()
                            for q in range(8):
                                h = hg * 8 + q; c = h // 2; po = (h % 2) * 64
                                mm(pb[:, q * 64:(q + 1) * 64], arz[h % 2][:, c, 0:128], Hb[:, c, :], True, False, [arz[h % 2], Hb], [pb])
                                mm(pb[:, q * 64:(q + 1) * 64], AKM[:, h, 0:128], tmv[:, h * 64:(h + 1) * 64], False, True, [AKM, tmv], [pb])
                            cp("scalar", X0b[:, hg * 8:hg * 8 + 8, :], pb[:].rearrange("p (h n) -> p h n", h=8), [pb], [X0b])
                        for hg in range(2):
                            pb = nextps()
                            for q in range(8):
                                h = hg * 8 + q
                                mm(pb[:, q * 64:(q + 1) * 64], Tt[:, h, :], X0b[:, h, :], True, True, [Tt, X0b], [pb])
                            cp("vector", Ub[:, hg * 8:hg * 8 + 8, :], pb[:].rearrange("p (h n) -> p h n", h=8), [pb], [Ub])
                        if need_y:
                            for hg in range(2):
                                pb = nextps()
                                for q in range(8):
                                    h = hg * 8 + q; c = h // 2; po = (h % 2) * 64
                                    mm(pb[:, q * 64:(q + 1) * 64], arz[h % 2][:, c, 128:256], Hb[:, c, :], True, False, [arz[h % 2], Hb], [pb])
                                    mm(pb[:, q * 64:(q + 1) * 64], ABM[:, h, 128:256], Ub[:, h, :], False, False, [ABM, Ub], [pb])
                                    mm(pb[:, q * 64:(q + 1) * 64], AKM[:, h, 128:256], tmv[:, h * 64:(h + 1) * 64], False, True, [AKM, tmv], [pb])
                                cp("scalar", ysb[:, hg * 512:(hg + 1) * 512], pb[:], [pb], [ysb])
                            S.dma(ST, Y[d][ci], ysb[:], [ysb], [bY[d][ci]])
                        pbs = (nextps(), nextps())
                        for h in range(16):
                            c = h // 2; po = (h % 2) * 64
                            pb = pbs[h % 2]
                            mm(pb[po:po + 64, c * 64:(c + 1) * 64], tmb[:, h * 64:(h + 1) * 64], Ub[:, h, :], True, False, [tmb, Ub], [pb])
                            mm(pb[po:po + 64, c * 64:(c + 1) * 64], tmk[:, h * 64:(h + 1) * 64], tmv[:, h * 64:(h + 1) * 64], False, True, [tmk, tmv], [pb])
                        for e_ in range(2):
                            po = e_ * 64
                            tt("vector", Htmp[po:po + 64, :, :], pbs[e_][po:po + 64, :].rearrange("p (c n) -> p c n", c=8), Hf[po:po + 64, :, :], ALU.add, [pbs[e_], Hf], [Htmp])
                        tt("vector", Hf[:], Htmp[:], pc[:].unsqueeze(2).broadcast_to([128, 8, 64]), ALU.mult, [Htmp, pc], [Hf])
                        cp("scalar", Hb[:], Hf[:], [Hf], [Hb])
                phase_end()

        def rwkv_p3(i, idx, ctx_readout):
            with ExitStack() as ph:
                wo = sbt(ph, [128, 8, D], BF16)
                with ExitStack() as sp:
                    stg = [sbt(sp, [128, 4096]) for _ in range(2)]
                    load_w(stg, wo, wo_in[idx], 128, D)
                    phase_end()
                WT = 512
                xt_ = [sbt(ph, [128, 8, WT]) for _ in range(2)]
                gt = sbt(ph, [128, 8, WT], BF16); bnt = sbt(ph, [128, 8, WT], BF16)
                pre = sbt(ph, [128, 8, WT]); preb = sbt(ph, [128, 8, WT], BF16)
                y0_ = [sbt(ph, [128, D]) for _ in range(2)]; y1_ = [sbt(ph, [128, D]) for _ in range(2)]
                ysq = sbt(ph, [128, D]); yn = sbt(ph, [128, D], BF16)
                st = sbt(ph, [128, 4, 16])
                tiles = ([(0, NCC)] if ctx_readout else []) + [(c, 4) for c in range(NCC, NCH, 4)]
                n = 0
                for tn, (c0, nch) in enumerate(tiles):
                    kind = chunk_kind(c0)
                    w = nch * 128
                    xt = xt_[tn % 2]
                    xb = [bX[c0 + j] for j in range(nch)]
                    S.dma(LD, xt[:, :, 0:w], fm(X, c0 * 128, w), xb, [xt])
                    S.dma(LD, gt[:, :, 0:w], fm(G, c0 * 128, w), [bG[c0 + j] for j in range(nch)], [gt])
                    S.dma(LD, bnt[:, :, 0:w], fm(BN, c0 * 128, w), [bG[c0 + j] for j in range(nch)], [bnt])
                    for j in range(nch):
                        ci = c0 + j
                        y0 = y0_[n % 2]; y1 = y1_[n % 2]
                        n += 1
                        S.dma(LD, y0[:], Y[0][ci], [bY[0][ci]], [y0]); S.dma(LD, y1[:], Y[1][ci], [bY[1][ci]], [y1])
                        tt("vector", y0[:], y0[:], y1[:], ALU.add, [y0, y1], [y0])
                        y3 = y0[:].rearrange("p (h n) -> p h n", h=16)
                        rsum("vector", st[:, 0, :], y3, [y0], [st])
                        tt("gpsimd", ysq[:], y0[:], y0[:], ALU.mult, [y0], [ysq])
                        rsum("vector", st[:, 1, :], ysq[:].rearrange("p (h n) -> p h n", h=16), [ysq], [st])
                        ts("vector", st[:, 0, :], st[:, 0, :], 1.0 / 64, ALU.mult, [st], [st])
                        tt("vector", st[:, 2, :], st[:, 0, :], st[:, 0, :], ALU.mult, [st], [st])
                        stt("vector", st[:, 1, :], st[:, 1, :], 1.0 / 64, st[:, 2, :], ALU.mult, ALU.subtract, [st], [st])
                        act(st[:, 1, :], st[:, 1, :], AF.Sqrt, [st, kst], [st], bias=kst[:, 1:2])
                        recip(st[:, 1, :], st[:, 1, :], [st], [st])
                        tt("vector", y3, y3, st[:, 0, :].unsqueeze(2).broadcast_to([128, 16, 64]), ALU.subtract, [y0, st], [y0])
                        tt("vector", yn[:].rearrange("p (h n) -> p h n", h=16), y3, st[:, 1, :].unsqueeze(2).broadcast_to([128, 16, 64]), ALU.mult, [y0, st], [yn])
                        pb = nextps()
                        pbb = pb[:].bitcast(BF16)
                        for c in range(8):
                            tr(pbb[:, c * 128:(c + 1) * 128], yn[:, c * 128:(c + 1) * 128], idb[:], [yn, idb], [pb])
                        for c in range(8):
                            act(pre[:, c, j * 128:(j + 1) * 128], pbb[:, c * 128:(c + 1) * 128], AF.Identity, [pb, vecs], [pre],
                                bias=vecs[:, vidx["lb_%d" % idx], c:c + 1], scale=vecs[:, vidx["lw_%d" % idx], c:c + 1])
                    tt("vector", pre[:, :, 0:w], pre[:, :, 0:w], bnt[:, :, 0:w], ALU.add, [pre, bnt], [pre])
                    tt("gpsimd", preb[:, :, 0:w], pre[:, :, 0:w], gt[:, :, 0:w], ALU.mult, [pre, gt], [preb])
                    for m in range(8):
                        pb = nextps()
                        for kc in range(8):
                            mm(pb[:, 0:w], wo[:, kc, m * 128:(m + 1) * 128], preb[:, kc, 0:w], kc == 0, kc == 7, [wo, preb], [pb])
                        stt("vector", xt[:, m, 0:w], pb[:, 0:w], lay[:, kind, 2, m:m + 1], xt[:, m, 0:w], ALU.mult, ALU.add, [pb, lay, xt], [xt])
                    S.dma(ST, fm(X, c0 * 128, w), xt[:, :, 0:w], [xt], xb)
                phase_end()

        def fnet_f1(do_ctx):
            with ExitStack() as ph:
                dftc = sbt(ph, [128, 2, 512], BF16)
                S.dma(LD, dftc[:], dftc_in, (), [dftc])
                xt_ = [sbt(ph, [128, 8, 128]) for _ in range(2)]
                hb = sbt(ph, [128, 8, 128], BF16)
                ycs_ = [sbt(ph, [128, 2, D], BF16) for _ in range(2)]
                sq_ = [sbt(ph, [128, 128]) for _ in range(2)]; rstd = sbt(ph, [128, 128])
                for ci in range(0 if do_ctx else NCC, NCH):
                    kind = chunk_kind(ci)
                    xt = xt_[ci % 2]; ycs = ycs_[ci % 2]
                    S.dma(LD, xt[:], fm(X, ci * 128, 128), [bX[ci]], [xt])
                    norm_mod(xt, 128, kind, 0, sq_, rstd, hb)
                    for g in range(4):
                        pb = nextps()
                        for kc in range(2):
                            mm(pb[:], hb[:, 2 * g + kc, :], dftc[:, kc, :], kc == 0, kc == 1, [hb, dftc], [pb])
                        cp("vector" if g % 2 == 0 else "scalar", ycs[:, :, g * 256:(g + 1) * 256], pb[:].rearrange("p (a n) -> p a n", a=2), [pb], [ycs])
                    S.dma(ST, YCS[ci], ycs[:], [ycs], [bYCS[ci]])
                phase_end()

        def fnet_f2(i, idx, do_ctx):
            with ExitStack() as ph:
                wo = sbt(ph, [128, 8, D], BF16)
                with ExitStack() as sp:
                    stg = [sbt(sp, [128, 4096]) for _ in range(2)]
                    load_w(stg, wo, fwo_in[idx], 128, D)
                    phase_end()
                WT = 512
                ycs_ = [sbt(ph, [128, 2, D], BF16) for _ in range(3)]
                dl_ = [sbt(ph, [128, 2, WT], BF16) for _ in range(3)]
                fb = sbt(ph, [128, 8, WT], BF16)
                xt_ = [sbt(ph, [128, 8, WT]) for _ in range(2)]
                jobs = []
                if do_ctx:
                    jobs.append((1, 0, NCC, dlc_in, 0, CTX))
                for k0 in range(0, SEQ, WT):
                    jobs.append((0, NCC, NCH - NCC, dll_in, k0, WT))
                n = 0
                for tn, (kind, cbase, nl, DL, k0, wk) in enumerate(jobs):
                    xt = xt_[tn % 2]
                    tok0 = cbase * 128 + k0
                    xb = [bX[tok0 // 128 + j] for j in range(wk // 128)]
                    S.dma(LD, xt[:, :, 0:wk], fm(X, tok0, wk), xb, [xt])
                    acc = [nextps() for _ in range(8)]
                    for li in range(nl):
                        ycs = ycs_[n % 3]; dl = dl_[n % 3]
                        n += 1
                        S.dma(LD, ycs[:], YCS[cbase + li], [bYCS[cbase + li]], [ycs])
                        S.dma(LD, dl[:, :, 0:wk], DL[:, li * 128:(li + 1) * 128, k0:k0 + wk].rearrange("a p k -> p a k"), (), [dl])
                        for m in range(8):
                            mm(acc[m][:, 0:wk], ycs[:, 0, m * 128:(m + 1) * 128], dl[:, 0, 0:wk], li == 0, False, [ycs, dl], [acc[m]])
                            mm(acc[m][:, 0:wk], ycs[:, 1, m * 128:(m + 1) * 128], dl[:, 1, 0:wk], False, li == nl - 1, [ycs, dl], [acc[m]])
                    for m in range(8):
                        cp("vector" if m % 2 == 0 else "scalar", fb[:, m, 0:wk], acc[m][:, 0:wk], [acc[m]], [fb])
                    for m in range(8):
                        pb = nextps()
                        for kc in range(8):
                            mm(pb[:, 0:wk], wo[:, kc, m * 128:(m + 1) * 128], fb[:, kc, 0:wk], kc == 0, kc == 7, [wo, fb], [pb])
                        stt("vector", xt[:, m, 0:wk], pb[:, 0:wk], lay[:, kind, 2, m:m + 1], xt[:, m, 0:wk], ALU.mult, ALU.add, [pb, lay, xt], [xt])
                    S.dma(ST, fm(X, tok0, wk), xt[:, :, 0:wk], [xt], xb)
                phase_end()

        def final_phase():
            with ExitStack() as ph:
                xt_ = [sbt(ph, [128, 8, 128]) for _ in range(2)]
                ot_ = [sbt(ph, [128, D]) for _ in range(2)]
                sq_ = [sbt(ph, [128, 128]) for _ in range(2)]; rstd = sbt(ph, [128, 128])
                for ci in range(NCC, NCH):
                    xt = xt_[ci % 2]; ot = ot_[ci % 2]
                    S.dma(LD, xt[:], fm(X, ci * 128, 128), [bX[ci]], [xt])
                    rms_rstd(xt, 128, sq_, rstd)
                    tt("vector", xt[:], xt[:], rstd[:, 0:128].unsqueeze(1).broadcast_to([128, 8, 128]), ALU.mult, [xt, rstd], [xt])
                    tt("gpsimd", xt[:], xt[:], bc_t(vcol("fin"), 128), ALU.mult, [xt, vecs], [xt])
                    for hf in range(2):
                        pb = nextps()
                        for q in range(4):
                            c = hf * 4 + q
                            tr(pb[:, q * 128:(q + 1) * 128], xt[:, c, :], idf[:], [xt, idf], [pb])
                        cp("vector" if hf == 0 else "scalar", ot[:, hf * 512:(hf + 1) * 512], pb[:], [pb], [ot])
                    S.dma(ST, out[(ci - NCC) * 128:(ci - NCC + 1) * 128, :], ot[:], [ot], ())
                phase_end()

        for i in range(DEPTH):
            mixer = i % 2
            idx = i // 2
            ctx_in = i <= last_a
            ctx_out = i < last_a
            set_layer(i)
            on = lambda p: phases is None or p in phases
            if mixer == 0:
                if on("p1"): rwkv_p1(i, idx)
                if on("p2"): rwkv_p2(ctx_out)
                if on("p3"): rwkv_p3(i, idx, ctx_out)
            else:
                if on("f1"): fnet_f1(ctx_out)
                if on("f2"): fnet_f2(i, idx, ctx_out)
            if on("mlp"): mlp_phase(i, ctx_out)
        import os
        if not os.environ.get('K_SKIPFINAL'):
            final_phase()
        print("instructions (incl waits):", S.ninst)
    return nc


def _col(v):
    return np.ascontiguousarray(np.asarray(v, np.float32).reshape(8, 128).T)


def host_consts(SEQ, CTX):
    bf = ml_dtypes.bfloat16
    c = {}
    c["idb"] = np.eye(128, dtype=np.float32).astype(bf)
    c["idf"] = np.eye(128, dtype=np.float32)
    c["onesf"] = np.ones((128, 128), np.float32)
    blk = np.zeros((128, 128), np.float32); blk[:64, :64] = 1; blk[64:, 64:] = 1
    c["blk"] = blk
    s = np.arange(128)[:, None]; t = np.arange(128)[None, :]
    mam = np.zeros((2, 128, 512), np.float32)
    for d, (st, inc) in enumerate((((s < t), (s <= t)), ((s > t), (s >= t)))):
        one = np.concatenate([st, inc], 1).astype(np.float32)
        mam[d] = np.concatenate([one, one], 1)
    c["mam"] = mam.astype(bf)
    mn = np.zeros((2, 128, 512), np.float32)
    mn[0] = np.tile((s > t).astype(np.float32), (1, 4))
    mn[1] = np.tile((s < t).astype(np.float32), (1, 4))
    c["mn"] = mn.astype(bf)
    cc = np.arange(256)
    ang = 2 * np.pi * np.outer(cc, cc) / 256.0
    dftc = np.concatenate([np.cos(ang), np.sin(ang)], 1) / 16.0
    c["dftc"] = np.ascontiguousarray(dftc.reshape(2, 128, 512).transpose(1, 0, 2)).astype(bf)

    def dl(L):
        l = np.arange(L, dtype=np.int64)
        prod = np.outer(l, l) % L
        ang = 2 * np.pi * prod / float(L)
        return np.stack([np.cos(ang), -np.sin(ang)]).astype(np.float32) / math.sqrt(L)
    c["dll"] = dl(SEQ).astype(bf)
    c["dlc"] = dl(CTX).astype(bf)
    return c


def host_shared(inp, DEPTH):
    f = lambda a: np.ascontiguousarray(np.asarray(a, np.float32))
    n_a = (DEPTH + 1) // 2
    vidx = vec_index(DEPTH)
    vecs = np.zeros((128, len(vidx), 8), np.float32)

    def put(name, v):
        vecs[:, vidx[name], :] = _col(v)
    for i in range(DEPTH):
        put("n1_%d" % i, inp["norm1_g"][i]); put("n2_%d" % i, inp["norm2_g"][i])
    put("fin", inp["final_g"])
    for a in range(n_a):
        for j in range(6):
            put("mu%d_%d" % (j, a), inp["rk_mu"][a, j])
        for d in range(2):
            put("w0_%d_%d" % (d, a), inp["rk_w0"][a, d]); put("a0_%d_%d" % (d, a), inp["rk_a0"][a, d])
        put("kk_%d" % a, inp["rk_kk"][a]); put("ka_%d" % a, inp["rk_ka"][a]); put("rk_%d" % a, np.asarray(inp["rk_rk"][a]).reshape(-1))
        put("lw_%d" % a, inp["rk_lnx_w"][a]); put("lb_%d" % a, inp["rk_lnx_b"][a])
        if a >= 1:
            put("v0_%d" % a, inp["rk_v0"][a - 1])
    sh = {}
    sh["vecs"] = vecs.reshape(128, -1)
    sh["modw"] = f(inp["mod_w"][:DEPTH]); sh["modb"] = np.ascontiguousarray(np.asarray(inp["mod_b"][:DEPTH], np.float32).reshape(DEPTH, 48, 128).transpose(2, 0, 1).reshape(128, DEPTH * 48))
    sh["w1m"] = f(inp["mlp_w1"][:DEPTH]); sh["w2m"] = f(inp["mlp_w2"][:DEPTH])
    sh["wr"] = f(inp["rk_wr"][:n_a]); sh["wk"] = f(inp["rk_wk"][:n_a]); sh["wv"] = f(inp["rk_wv"][:n_a]); sh["wo"] = f(inp["rk_wo"][:n_a])
    cat1 = lambda w: f(np.concatenate([np.asarray(w)[:n_a, 0], np.asarray(w)[:n_a, 1]], axis=2))
    cat2 = lambda w: f(np.concatenate([np.asarray(w)[:n_a, 0], np.asarray(w)[:n_a, 1]], axis=1))
    sh["lw1"] = cat1(inp["rk_w1"]); sh["lw2"] = cat2(inp["rk_w2"]); sh["la1"] = cat1(inp["rk_a1"]); sh["la2"] = cat2(inp["rk_a2"])
    sh["lg1"] = f(inp["rk_g1"][:n_a]); sh["lg2"] = f(inp["rk_g2"][:n_a])
    if n_a >= 2:
        sh["lv1"] = f(inp["rk_v1"][:n_a - 1]); sh["lv2"] = f(inp["rk_v2"][:n_a - 1])
    else:
        sh["lv1"] = np.zeros((1, D, 32), np.float32); sh["lv2"] = np.zeros((1, 32, D), np.float32)
    n_b = DEPTH // 2
    sh["fwo"] = f(inp["ft_wo"][:n_b]) if n_b >= 1 else np.zeros((1, D, D), np.float32)
    return sh


def run(inp, SEQ, CTX, DEPTH, n_cores, dbg=False, phases=None):
    nc = build(SEQ, CTX, DEPTH, dbg, phases)
    consts = host_consts(SEQ, CTX)
    sh = host_shared(inp, DEPTH)
    in_maps = []
    for b in range(n_cores):
        m = dict(consts); m.update(sh)
        m["xin"] = np.ascontiguousarray(np.asarray(inp["x"][b], np.float32))
        m["cin"] = np.ascontiguousarray(np.asarray(inp["ctx"][b], np.float32))
        ccv = np.zeros((128, 8, 2), np.float32)
        ccv[:, :, 0] = _col(inp["c"][b]); ccv[:, :, 1] = _col(inp["c_ctx"])
        m["cc"] = ccv.reshape(128, 16)
        in_maps.append(m)
    res = run_bass_kernel_spmd(nc, in_maps, core_ids=list(range(n_cores)))
    return res


def kernel(**inputs):
    x = np.asarray(inputs["x"])
    B, SEQ, _ = x.shape
    CTX = np.asarray(inputs["ctx"]).shape[1]
    DEPTH = np.asarray(inputs["mod_w"]).shape[0]
    res = run(inputs, SEQ, CTX, DEPTH, B)
    return np.stack([np.asarray(r["out"], np.float32) for r in res.results], axis=0)
```

```python
import math
import numpy as np
import ml_dtypes
from contextlib import ExitStack
import concourse.bass as bass
import concourse.mybir as mybir
from concourse.bass_utils import run_bass_kernel_spmd

F32 = mybir.dt.float32
BF16 = mybir.dt.bfloat16
AF = mybir.ActivationFunctionType
ALU = mybir.AluOpType
AX = mybir.AxisListType

D = 1024
NK = 8
NH = 16
C0 = math.exp(-0.5)
NMOD = 6


class Buf:
    __slots__ = ("lw", "rd", "t")

    def __init__(self, t=None):
        self.lw = None
        self.rd = {}
        self.t = t

    def __getitem__(self, k):
        return self.t[k]


class Sched:
    NDMA = 24

    def __init__(self, nc, stack):
        self.nc = nc
        self.names = ["tensor", "vector", "scalar", "gpsimd", "sync"]
        self.sem = {n: stack.enter_context(nc.semaphore("s_" + n)) for n in self.names}
        self.cnt = {n: 0 for n in self.names}
        self.prog = {n: [] for n in self.names}
        self.waited = {n: {} for n in self.names}
        self.dsem = [stack.enter_context(nc.semaphore("d%d" % i)) for i in range(self.NDMA)]
        self.dcnt = [0] * self.NDMA
        self.dnext = 0
        self.ninst = 0

    def _semobj(self, key):
        return self.sem[key] if isinstance(key, str) else self.dsem[key]

    def _collect(self, eng, reads, writes):
        need = {}

        def add(k, v):
            if k == eng and eng == "tensor":
                return
            if need.get(k, 0) < v:
                need[k] = v
        for t in reads:
            if t.lw is not None:
                add(*t.lw)
        for t in writes:
            if t.lw is not None:
                add(*t.lw)
            for k, v in t.rd.items():
                if k != eng:
                    add(k, v)
        w = self.waited[eng]
        out = []
        for k, v in need.items():
            if w.get(k, 0) >= v:
                continue
            w[k] = v
            out.append((k, v))
        return out

    def op(self, eng, fn, reads=(), writes=()):
        waits = self._collect(eng, reads, writes)
        self.cnt[eng] += 1
        v = self.cnt[eng]
        self.prog[eng].append((waits, fn, (eng, 1)))
        self.ninst += 1 + len(waits)
        for t in reads:
            t.rd[eng] = v
        for t in writes:
            t.lw = (eng, v)
            t.rd = {}

    def dma(self, q, out_ap, in_ap, reads=(), writes=()):
        slot = self.dnext
        self.dnext = (self.dnext + 1) % self.NDMA
        waits = self._collect(q, reads, writes)
        prev = self.dcnt[slot]
        w = self.waited[q]
        if prev > 0 and w.get(slot, 0) < prev:
            w[slot] = prev
            waits.append((slot, prev))
        self.dcnt[slot] += 16
        v = self.dcnt[slot]
        self.prog[q].append((waits, lambda e: e.dma_start(out=out_ap, in_=in_ap), (slot, 16)))
        self.ninst += 1 + len(waits)
        for t in reads:
            t.rd[slot] = v
        for t in writes:
            t.lw = (slot, v)
            t.rd = {}

    def barrier(self):
        for n in self.names:
            waits = []
            w = self.waited[n]
            for slot in range(self.NDMA):
                if self.dcnt[slot] > w.get(slot, 0):
                    w[slot] = self.dcnt[slot]
                    waits.append((slot, self.dcnt[slot]))
            for m in self.names:
                if m != n and self.cnt[m] > w.get(m, 0):
                    w[m] = self.cnt[m]
                    waits.append((m, self.cnt[m]))
            if waits:
                self.prog[n].append((waits, None, None))
                self.ninst += len(waits)

    def emit(self):
        nc = self.nc
        with nc.Block() as block:
            for n in self.names:
                prog = self.prog[n]
                if not prog:
                    continue

                def body(e, prog=prog):
                    for waits, fn, inc in prog:
                        for k, v in waits:
                            e.wait_ge(self._semobj(k), v)
                        if fn is not None:
                            fn(e).then_inc(self._semobj(inc[0]), inc[1])
                getattr(block, n)(body)
        self.prog = {n: [] for n in self.names}


def vec_index(DEPTH):
    n_a = (DEPTH + 1) // 2
    idx = {}
    names = []
    for i in range(DEPTH):
        names += ["n1_%d" % i, "n2_%d" % i]
    names.append("fin")
    for a in range(n_a):
        names += ["mu%d_%d" % (j, a) for j in range(6)]
        names += ["w0_%d_%d" % (d, a) for d in range(2)] + ["a0_%d_%d" % (d, a) for d in range(2)]
        names += ["kk_%d" % a, "ka_%d" % a, "rk_%d" % a, "lw_%d" % a, "lb_%d" % a]
        if a >= 1:
            names.append("v0_%d" % a)
    for k, n in enumerate(names):
        idx[n] = k
    return idx


def build(SEQ, CTX, DEPTH, dbg=False, phases=None):
    assert CTX == 256 and SEQ % 512 == 0
    nc = bass.Bass("TRN2", target_bir_lowering=False)
    TT = CTX + SEQ
    NCH = TT // 128
    NCC = CTX // 128
    last_a = ((DEPTH - 1) // 2) * 2
    n_a = (DEPTH + 1) // 2
    n_b = DEPTH // 2
    n_v = max(n_a - 1, 1)
    vidx = vec_index(DEPTH)
    NV = len(vidx)

    def din(name, shape, dt=F32):
        return nc.dram_tensor(name, list(shape), dt, kind="ExternalInput").ap()

    def dscr(name, shape, dt=F32):
        return nc.dram_tensor(name, list(shape), dt, kind="ExternalOutput" if dbg else "Internal").ap()

    xin = din("xin", [SEQ, D]); cin = din("cin", [CTX, D]); ccin = din("cc", [128, 16])
    modw = din("modw", [DEPTH, D, NMOD * D]); modb = din("modb", [128, DEPTH * 48])
    vecs_in = din("vecs", [128, NV * 8])
    w1m = din("w1m", [DEPTH, D, 4 * D]); w2m = din("w2m", [DEPTH, 4 * D, D])
    wr_in = din("wr", [n_a, D, D]); wk_in = din("wk", [n_a, D, D]); wv_in = din("wv", [n_a, D, D]); wo_in = din("wo", [n_a, D, D])
    lw1_in = din("lw1", [n_a, D, 128]); lw2_in = din("lw2", [n_a, 128, D])
    la1_in = din("la1", [n_a, D, 128]); la2_in = din("la2", [n_a, 128, D])
    lg1_in = din("lg1", [n_a, D, 128]); lg2_in = din("lg2", [n_a, 128, D])
    lv1_in = din("lv1", [n_v, D, 32]); lv2_in = din("lv2", [n_v, 32, D])
    fwo_in = din("fwo", [max(n_b, 1), D, D])
    idb_in = din("idb", [128, 128], BF16); idf_in = din("idf", [128, 128]); ones_in = din("onesf", [128, 128]); blk_in = din("blk", [128, 128])
    mam_in = din("mam", [2, 128, 512], BF16); mn_in = din("mn", [2, 128, 512], BF16)
    dftc_in = din("dftc", [128, 2, 512], BF16)
    dll_in = din("dll", [2, SEQ, SEQ], BF16); dlc_in = din("dlc", [2, CTX, CTX], BF16)
    out = nc.dram_tensor("out", [SEQ, D], F32, kind="ExternalOutput").ap()

    X = dscr("X", [D, TT])
    OPA = [dscr("OPA%d" % d, [NCH, 128, 8, 256], BF16) for d in range(2)]
    OPB = [dscr("OPB%d" % d, [NCH, 128, 8, 128], BF16) for d in range(2)]
    OPK = [dscr("OPK%d" % d, [NCH, 128, 8, 128], BF16) for d in range(2)]
    TMB = [dscr("TMB%d" % d, [NCH, 128, D], BF16) for d in range(2)]
    TMK = [dscr("TMK%d" % d, [NCH, 128, D], BF16) for d in range(2)]
    TMV = dscr("TMV", [NCH, 128, D], BF16)
    PCD = [dscr("PCD%d" % d, [NCH, 128, 8]) for d in range(2)]
    G = dscr("G", [D, TT], BF16); BN = dscr("BN", [D, TT], BF16); VF = dscr("VF", [D, TT])
    Y = [dscr("Y%d" % d, [NCH, 128, D]) for d in range(2)]
    YCS = dscr("YCS", [NCH, 128, 2, D], BF16)

    def cb():
        return [Buf() for _ in range(NCH)]
    bX = cb(); bOP = [cb(), cb()]; bG = cb(); bVF = cb(); bY = [cb(), cb()]; bYCS = cb()

    def fm(ap, t0, w):
        return ap[:, t0:t0 + w].rearrange("(c p) t -> p c t", p=128)

    with ExitStack() as gs:
        S = Sched(nc, gs)
        uid = [0]

        def sbt(stk, shape, dt=F32):
            uid[0] += 1
            return Buf(stk.enter_context(nc.sbuf_tensor("t%d" % uid[0], list(shape), dt)))

        PS = [Buf(gs.enter_context(nc.psum_tensor("ps%d" % i, [128, 512], F32))) for i in range(8)]
        psn = [0]

        def nextps():
            b = PS[psn[0] % 8]
            psn[0] += 1
            return b

        def mm(o, lhsT, rhs, start, stop, R, Wr):
            S.op("tensor", lambda e: e.matmul(o, lhsT=lhsT, rhs=rhs, start=start, stop=stop), R, Wr)

        def tr(o, i, ident, R, Wr):
            S.op("tensor", lambda e: e.transpose(out=o, in_=i, identity=ident), R, Wr)

        def tt(eng, o, a, b, op, R, Wr):
            S.op(eng, lambda e: e.tensor_tensor(out=o, in0=a, in1=b, op=op), R, Wr)

        def ts(eng, o, a, s1, op0, R, Wr, s2=None, op1=None):
            if op1 is None:
                S.op(eng, lambda e: e.tensor_scalar(out=o, in0=a, scalar1=s1, scalar2=None, op0=op0), R, Wr)
            else:
                S.op(eng, lambda e: e.tensor_scalar(out=o, in0=a, scalar1=s1, scalar2=s2, op0=op0, op1=op1), R, Wr)

        def stt(eng, o, a, s, b, op0, op1, R, Wr):
            S.op(eng, lambda e: e.scalar_tensor_tensor(out=o, in0=a, scalar=s, in1=b, op0=op0, op1=op1), R, Wr)

        def act(o, i, func, R, Wr, bias=None, scale=1.0):
            if bias is None:
                S.op("scalar", lambda e: e.activation(out=o, in_=i, func=func, scale=scale), R, Wr)
            else:
                S.op("scalar", lambda e: e.activation(out=o, in_=i, func=func, bias=bias, scale=scale), R, Wr)

        def cp(eng, o, i, R, Wr):
            if eng == "scalar":
                S.op(eng, lambda e: e.copy(out=o, in_=i), R, Wr)
            else:
                S.op(eng, lambda e: e.tensor_copy(out=o, in_=i), R, Wr)

        def rsum(eng, o, i, R, Wr):
            S.op(eng, lambda e: e.reduce_sum(out=o, in_=i, axis=AX.X), R, Wr)

        def recip(o, i, R, Wr):
            S.op("vector", lambda e: e.reciprocal(out=o, in_=i), R, Wr)

        def mset(eng, o, val, Wr):
            S.op(eng, lambda e: e.memset(o, val), (), Wr)

        def scan(o, d0, d1, R, Wr):
            S.op("vector", lambda e: e.tensor_tensor_scan(out=o, data0=d0, data1=d1, initial=0.0, op0=ALU.mult, op1=ALU.add), R, Wr)

        LD, ST = "sync", "gpsimd"

        def phase_end():
            S.barrier()
            S.emit()

        idb = sbt(gs, [128, 128], BF16); idf = sbt(gs, [128, 128]); onesf = sbt(gs, [128, 128]); blk = sbt(gs, [128, 128])
        mam = sbt(gs, [128, 2, 512], BF16); mn = sbt(gs, [128, 2, 512], BF16)
        vecs = sbt(gs, [128, NV, 8]); cc = sbt(gs, [128, 8, 2]); sc = sbt(gs, [128, 8, 2])
        modT = sbt(gs, [128, DEPTH * 2 * 48]); kst = sbt(gs, [128, 4])
        lay = sbt(gs, [128, 2, 6, 8])
        S.dma(LD, idb[:], idb_in, (), [idb]); S.dma(LD, idf[:], idf_in, (), [idf])
        S.dma(LD, onesf[:], ones_in, (), [onesf]); S.dma(LD, blk[:], blk_in, (), [blk])
        S.dma(LD, mam[:], mam_in.rearrange("d p n -> p d n"), (), [mam]); S.dma(LD, mn[:], mn_in.rearrange("d p n -> p d n"), (), [mn])
        S.dma(LD, vecs[:], vecs_in.rearrange("p (v c) -> p v c", c=8), (), [vecs])
        S.dma(LD, cc[:], ccin.rearrange("p (c w) -> p c w", w=2), (), [cc])
        mset("vector", kst[:, 0:1], 1e-6, [kst]); mset("vector", kst[:, 1:2], 64e-5, [kst]); mset("vector", kst[:, 2:3], 0.0, [kst])
        act(sc[:], cc[:], AF.Silu, [cc], [sc])

        def vcol(name):
            return vecs[:, vidx[name], :]

        def bc_t(col, w):
            return col.unsqueeze(2).broadcast_to([128, 8, w])

        with ExitStack() as ph:
            wt = [sbt(ph, [128, 8, 512]) for _ in range(2)]
            mbT = sbt(ph, [128, DEPTH, 48])
            S.dma(LD, mbT[:], modb.rearrange("p (i j) -> p i j", i=DEPTH), (), [mbT])
            modT4 = modT[:].rearrange("p (i w j) -> p i w j", i=DEPTH, w=2)
            import os
            for i in range(DEPTH if not os.environ.get('K_SKIPMOD') else 0):
                for n in range(12):
                    w = wt[n % 2]
                    S.dma(LD, w[:], modw[i, :, n * 512:(n + 1) * 512].rearrange("(c p) n -> p c n", p=128), (), [w])
                    pb = nextps()
                    for q in range(4):
                        for kc in range(8):
                            mm(pb[:, q * 2:(q + 1) * 2], w[:, kc, q * 128:(q + 1) * 128], sc[:, kc, :], kc == 0, kc == 7, [sc, w], [pb])
                    cp("vector", modT4[:, i, :, n * 4:(n + 1) * 4], pb[:, 0:8].rearrange("p (q w) -> p w q", w=2), [pb], [modT])
                for w_ in range(2):
                    tt("vector", modT4[:, i, w_, :], modT4[:, i, w_, :], mbT[:, i, :], ALU.add, [modT, mbT], [modT])
            xt_ = [sbt(ph, [128, D]) for _ in range(2)]
            xo_ = [sbt(ph, [128, 8, 128]) for _ in range(2)]
            for ci in range(NCH):
                xt = xt_[ci % 2]; xo = xo_[ci % 2]
                src = cin[ci * 128:(ci + 1) * 128, :] if ci < NCC else xin[(ci - NCC) * 128:(ci - NCC + 1) * 128, :]
                S.dma(LD, xt[:], src, (), [xt])
                for hf in range(2):
                    pb = nextps()
                    for q in range(4):
                        c = hf * 4 + q
                        tr(pb[:, q * 128:(q + 1) * 128], xt[:, c * 128:(c + 1) * 128], idf[:], [xt, idf], [pb])
                    cp("vector" if hf == 0 else "scalar", xo[:, hf * 4:(hf + 1) * 4, :], pb[:].rearrange("p (q t) -> p q t", q=4), [pb], [xo])
                S.dma(ST, fm(X, ci * 128, 128), xo[:], [xo], [bX[ci]])
            phase_end()

        def set_layer(i):
            for kind in range(2):
                base = (i * 2 + kind) * 48
                m = lambda j: modT[:, base + j * 8: base + (j + 1) * 8]
                for half, gname in ((0, "n1_%d" % i), (1, "n2_%d" % i)):
                    stt("vector", lay[:, kind, half * 3 + 0, :], m(half * 3 + 1), 1.0, vcol(gname), ALU.add, ALU.mult, [modT, vecs], [lay])
                    cp("vector", lay[:, kind, half * 3 + 1, :], m(half * 3 + 0), [modT], [lay])
                    cp("vector", lay[:, kind, half * 3 + 2, :], m(half * 3 + 2), [modT], [lay])

        def load_w(stk_tiles, dst, src_ap, rows_per, ncols):
            stg = stk_tiles
            KC = src_ap.shape[0] // 128
            step = max(1, 4096 // ncols)
            k = 0
            i = 0
            while k < KC:
                n = min(step, KC - k)
                s = stg[i % 2]
                sv = s[:, 0:n * ncols].rearrange("p (c n) -> p c n", c=n)
                S.dma(LD, sv, src_ap[k * 128:(k + n) * 128, :].rearrange("(c p) n -> p c n", p=128), (), [s])
                cp("gpsimd" if i % 2 == 0 else "vector", dst[:, k:k + n, :], sv, [s], [dst])
                k += n
                i += 1

        def load_small(stg, dst_ap, dstbuf, src_ap, rows, ncols):
            s = stg[0]
            S.dma(LD, s[0:rows, 0:ncols], src_ap, (), [s])
            cp("vector", dst_ap, s[0:rows, 0:ncols], [s], [dstbuf])

        def rms_rstd(x, w, sq_, rstd):
            pb = nextps()
            for c in range(8):
                sq = sq_[c % 2]
                act(sq[:, 0:w], x[:, c, 0:w], AF.Square, [x], [sq])
                mm(pb[:, 0:w], onesf[:], sq[:, 0:w], c == 0, c == 7, [onesf, sq], [pb])
            act(rstd[:, 0:w], pb[:, 0:w], AF.Sqrt, [pb, kst], [rstd], bias=kst[:, 0:1], scale=1.0 / D)
            recip(rstd[:, 0:w], rstd[:, 0:w], [rstd], [rstd])

        def norm_mod(x, w, kind, half, sq_, rstd, dst, dst_off=0):
            rms_rstd(x, w, sq_, rstd)
            tt("vector", x[:, :, 0:w], x[:, :, 0:w], rstd[:, 0:w].unsqueeze(1).broadcast_to([128, 8, w]), ALU.mult, [x, rstd], [x])
            for c in range(8):
                act(dst[:, c, dst_off:dst_off + w], x[:, c, 0:w], AF.Identity, [x, lay], [dst],
                    bias=lay[:, kind, half * 3 + 1, c:c + 1], scale=lay[:, kind, half * 3 + 0, c:c + 1])

        def chunk_kind(ci):
            return 1 if ci < NCC else 0

        def mlp_phase(i, do_ctx):
            with ExitStack() as ph:
                w1 = sbt(ph, [128, 8, 4 * D], BF16); w2 = sbt(ph, [128, 32, D], BF16)
                with ExitStack() as sp:
                    stg = [sbt(sp, [128, 4096]) for _ in range(2)]
                    load_w(stg, w1, w1m[i], 128, 4 * D)
                    load_w(stg, w2, w2m[i], 128, D)
                    phase_end()
                W = 256
                xt_ = [sbt(ph, [128, 8, W]) for _ in range(2)]
                xn = sbt(ph, [128, 8, W]); hb = sbt(ph, [128, 8, W], BF16); hid = sbt(ph, [128, 32, W], BF16)
                tmp_ = [sbt(ph, [128, 512]) for _ in range(2)]
                sq_ = [sbt(ph, [128, W]) for _ in range(2)]; rstd = sbt(ph, [128, W])
                tiles = list(range(0 if do_ctx else NCC, NCH, 2))
                for n, c0 in enumerate(tiles):
                    kind = chunk_kind(c0)
                    xt = xt_[n % 2]
                    S.dma(LD, xt[:], fm(X, c0 * 128, W), [bX[c0], bX[c0 + 1]], [xt])
                    cp("gpsimd", xn[:], xt[:], [xt], [xn])
                    norm_mod(xn, W, kind, 1, sq_, rstd, hb)
                    for mg in range(16):
                        pb = nextps()
                        for q in range(2):
                            m = mg * 2 + q
                            for kc in range(8):
                                mm(pb[:, q * W:(q + 1) * W], w1[:, kc, m * 128:(m + 1) * 128], hb[:, kc, :], kc == 0, kc == 7, [w1, hb], [pb])
                        tp = tmp_[mg % 2]
                        act(tp[:], pb[:], AF.Relu, [pb], [tp])
                        tt("vector" if mg % 2 == 0 else "gpsimd", hid[:, mg * 2:mg * 2 + 2, :], tp[:].rearrange("p (q t) -> p q t", q=2),
                           tp[:].rearrange("p (q t) -> p q t", q=2), ALU.mult, [tp], [hid])
                    for mg in range(4):
                        pb = nextps()
                        for q in range(2):
                            m = mg * 2 + q
                            for kc in range(32):
                                mm(pb[:, q * W:(q + 1) * W], w2[:, kc, m * 128:(m + 1) * 128], hid[:, kc, :], kc == 0, kc == 31, [w2, hid], [pb])
                        for q in range(2):
                            m = mg * 2 + q
                            stt("vector", xt[:, m, :], pb[:, q * W:(q + 1) * W], lay[:, kind, 5, m:m + 1], xt[:, m, :], ALU.mult, ALU.add, [pb, lay, xt], [xt])
                    S.dma(ST, fm(X, c0 * 128, W), xt[:], [xt], [bX[c0], bX[c0 + 1]])
                phase_end()

        def rwkv_p1(i, idx):
            with ExitStack() as ph:
                wr = sbt(ph, [128, 8, D], BF16); wk = sbt(ph, [128, 8, D], BF16); wv = sbt(ph, [128, 8, D], BF16)
                lw1 = sbt(ph, [128, 8, 128], BF16); la1 = sbt(ph, [128, 8, 128], BF16); lg1 = sbt(ph, [128, 8, 128], BF16); lv1 = sbt(ph, [128, 8, 32], BF16)
                lw2 = sbt(ph, [128, D], BF16); la2 = sbt(ph, [128, D], BF16); lg2 = sbt(ph, [128, D], BF16); lv2 = sbt(ph, [32, D], BF16)
                with ExitStack() as sp:
                    stg = [sbt(sp, [128, 4096]) for _ in range(2)]
                    load_w(stg, wr, wr_in[idx], 128, D); load_w(stg, wk, wk_in[idx], 128, D); load_w(stg, wv, wv_in[idx], 128, D)
                    load_w(stg, lw1, lw1_in[idx], 128, 128); load_w(stg, la1, la1_in[idx], 128, 128); load_w(stg, lg1, lg1_in[idx], 128, 128)
                    load_small(stg, lw2[:], lw2, lw2_in[idx], 128, D); load_small(stg, la2[:], la2, la2_in[idx], 128, D)
                    load_small(stg, lg2[:], lg2, lg2_in[idx], 128, D)
                    if idx >= 1:
                        load_w(stg, lv1, lv1_in[idx - 1], 128, 32)
                        load_small(stg, lv2[:], lv2, lv2_in[idx - 1], 32, D)
                    phase_end()
                WH = 256
                xh = sbt(ph, [128, 8, WH]); sq_ = [sbt(ph, [128, WH]) for _ in range(2)]; rstd = sbt(ph, [128, WH])
                Sx = [sbt(ph, [128, 8, 128]) for _ in range(7)]
                rkv2 = [[sbt(ph, [128, 8, 128]) for _ in range(4)] for _ in range(2)]
                mtmp = [sbt(ph, [128, 8, 128]) for _ in range(2)]
                mix = [sbt(ph, [128, 8, 128], BF16) for _ in range(6)]
                lb_ = [sbt(ph, [128, 128], BF16) for _ in range(4)]
                ar = [sbt(ph, [128, 8, 256], BF16) for _ in range(2)]
                bt = [sbt(ph, [128, 8, 128], BF16) for _ in range(2)]
                kt = [sbt(ph, [128, 8, 128], BF16) for _ in range(2)]
                tms = [sbt(ph, [128, D], BF16) for _ in range(2)]
                gb = sbt(ph, [128, 8, 128], BF16); bnb = sbt(ph, [128, 8, 128], BF16); vb = sbt(ph, [128, 8, 128], BF16)
                pct = [sbt(ph, [128, 8]) for _ in range(2)]
                omka = sbt(ph, [128, 8]); tott = sbt(ph, [128, 8])
                ts("vector", omka[:], vcol("ka_%d" % idx), -1.0, ALU.mult, [vecs], [omka], 1.0, ALU.add)
                tmc = [0]

                def proj_big(w, mx, dst, dst_eng):
                    for mg in range(2):
                        pb = nextps()
                        for q in range(4):
                            m = mg * 4 + q
                            for kc in range(8):
                                mm(pb[:, q * 128:(q + 1) * 128], w[:, kc, m * 128:(m + 1) * 128], mx[:, kc, :], kc == 0, kc == 7, [w, mx], [pb])
                        cp(dst_eng, dst[:, mg * 4:(mg + 1) * 4, :], pb[:].rearrange("p (q t) -> p q t", q=4), [pb], [dst])

                def lora1(w, mx, ncol, func, dstb):
                    pb = nextps()
                    for kc in range(8):
                        mm(pb[0:ncol, 0:128], w[:, kc, 0:ncol], mx[:, kc, :], kc == 0, kc == 7, [w, mx], [pb])
                    act(dstb[0:ncol, :], pb[0:ncol, 0:128], func, [pb], [dstb])

                def lora2(w2, lo, hi, lbuf, bias_name, dst):
                    for mg in range(2):
                        pb = nextps()
                        for q in range(4):
                            m = mg * 4 + q
                            mm(pb[:, q * 128:(q + 1) * 128], w2[lo:hi, m * 128:(m + 1) * 128], lbuf[lo:hi, :], True, True, [w2, lbuf], [pb])
                        for q in range(4):
                            m = mg * 4 + q
                            if bias_name is None:
                                cp("vector", dst[:, m, :], pb[:, q * 128:(q + 1) * 128], [pb], [dst])
                            else:
                                act(dst[:, m, :], pb[:, q * 128:(q + 1) * 128], AF.Sigmoid, [pb, vecs], [dst], bias=vecs[:, vidx[bias_name], m:m + 1])

                def headsum(src, dstfn):
                    for mg in range(2):
                        pb = nextps()
                        for q in range(4):
                            c = mg * 4 + q
                            mm(pb[:, q * 128:(q + 1) * 128], blk[:], src[:, c, :], True, True, [blk, src], [pb])
                        dstfn(mg, pb)

                def to_tm(src, dst_dram, dbuf):
                    pb = nextps()
                    pbb = pb[:].bitcast(BF16)
                    for c in range(8):
                        tr(pbb[:, c * 128:(c + 1) * 128], src[:, c, :], idb[:], [src, idb], [pb])
                    t = tms[tmc[0] % 2]
                    tmc[0] += 1
                    cp("scalar", t[:], pbb, [pb], [t])
                    S.dma(ST, dst_dram, t[:], [t], [dbuf])

                for ci in range(NCH):
                    r_t, k_t, v_t, kkn = rkv2[ci % 2]
                    kind = chunk_kind(ci)
                    t0 = ci * 128
                    lo_lim, hi_lim = (0, CTX) if kind == 1 else (CTX, TT)
                    lo = max(t0 - 64, lo_lim); hi = min(t0 + 192, hi_lim)
                    o0 = lo - (t0 - 64); o1 = hi - (t0 - 64)
                    if o0 > 0:
                        mset("gpsimd", xh[:, :, 0:o0], 0.0, [xh])
                    if o1 < WH:
                        mset("gpsimd", xh[:, :, o1:WH], 0.0, [xh])
                    S.dma(LD, xh[:, :, o0:o1], fm(X, lo, hi - lo), [bX[c] for c in range(lo // 128, (hi - 1) // 128 + 1)], [xh])
                    norm_mod(xh, WH, kind, 0, sq_, rstd, xh)
                    if o0 > 0:
                        mset("gpsimd", xh[:, :, 0:o0], 0.0, [xh])
                    if o1 < WH:
                        mset("gpsimd", xh[:, :, o1:WH], 0.0, [xh])
                    xx = Sx[0]
                    hc = xh[:, :, 64:192]
                    if kind == 0:
                        tt("vector", xx[:, 0:2, :], xh[:, 0:2, 63:191], xh[:, 0:2, 64:192], ALU.subtract, [xh], [xx])
                        tt("vector", xx[:, 2:4, :], xh[:, 2:4, 65:193], xh[:, 2:4, 64:192], ALU.subtract, [xh], [xx])
                        tt("gpsimd", xx[:, 4:6, :], xh[:, 4:6, 0:128], xh[:, 4:6, 64:192], ALU.subtract, [xh], [xx])
                        tt("gpsimd", xx[:, 6:8, :], xh[:, 6:8, 128:256], xh[:, 6:8, 64:192], ALU.subtract, [xh], [xx])
                        for j in (0, 64):
                            ts("vector", xx[:, 0:2, j:j + 1], xh[:, 0:2, 64 + j:65 + j], -1.0, ALU.mult, [xh, xx], [xx])
                            ts("vector", xx[:, 2:4, j + 63:j + 64], xh[:, 2:4, 127 + j:128 + j], -1.0, ALU.mult, [xh, xx], [xx])
                    else:
                        tt("vector", xx[:, 0:4, :], xh[:, 0:4, 63:191], xh[:, 0:4, 64:192], ALU.subtract, [xh], [xx])
                        tt("gpsimd", xx[:, 4:8, :], xh[:, 4:8, 65:193], xh[:, 4:8, 64:192], ALU.subtract, [xh], [xx])
                    for j in range(6):
                        eng = "vector" if j % 2 == 0 else "gpsimd"
                        tmp = mtmp[j % 2]
                        tt(eng, tmp[:], xx[:], bc_t(vcol("mu%d_%d" % (j, idx)), 128), ALU.mult, [xx, vecs], [tmp])
                        tt(eng, mix[j][:], tmp[:], hc, ALU.add, [tmp, xh], [mix[j]])
                    proj_big(wr, mix[0], r_t, "vector"); proj_big(wk, mix[2], k_t, "scalar"); proj_big(wv, mix[3], v_t, "vector")
                    wlb, alb, glb, vlb = lb_
                    lora1(lw1, mix[1], 128, AF.Tanh, wlb)
                    lora1(la1, mix[4], 128, AF.Identity, alb)
                    lora1(lg1, mix[5], 128, AF.Sigmoid, glb)
                    lora2(lg2, 0, 128, glb, None, Sx[1])
                    cp("gpsimd", gb[:], Sx[1][:], [Sx[1]], [gb])
                    S.dma(ST, fm(G, t0, 128), gb[:], [gb], [bG[ci]])
                    if idx >= 1:
                        lora1(lv1, mix[3], 32, AF.Identity, vlb)
                        sv = Sx[1]; vf = Sx[2]
                        lora2(lv2, 0, 32, vlb, "v0_%d" % idx, sv)
                        S.dma(LD, vf[:], fm(VF, t0, 128), [bVF[ci]], [vf])
                        tt("vector", vf[:], vf[:], v_t[:], ALU.subtract, [vf, v_t], [vf])
                        tt("vector", vf[:], vf[:], sv[:], ALU.mult, [vf, sv], [vf])
                        tt("vector", v_t[:], v_t[:], vf[:], ALU.add, [vf, v_t], [v_t])
                    else:
                        S.dma(ST, fm(VF, t0, 128), v_t[:], [v_t], [bVF[ci]])
                    cp("gpsimd", vb[:], v_t[:], [v_t], [vb])
                    to_tm(vb, TMV[ci], bOP[0][ci])
                    kkw = Sx[1]; sqk = Sx[2]
                    tt("vector", kkw[:], k_t[:], bc_t(vcol("kk_%d" % idx), 128), ALU.mult, [k_t, vecs], [kkw])
                    act(sqk[:], kkw[:], AF.Square, [kkw], [sqk])
                    rn = Sx[3]

                    def kkfin(mg, pb):
                        act(rn[:, mg * 4:(mg + 1) * 4, :], pb[:].rearrange("p (q t) -> p q t", q=4), AF.Sqrt, [pb], [rn])
                    headsum(sqk, kkfin)
                    ts("vector", rn[:], rn[:], 1e-12, ALU.max, [rn], [rn])
                    recip(rn[:], rn[:], [rn], [rn])
                    tt("vector", kkn[:], kkw[:], rn[:], ALU.mult, [kkw, rn], [kkn])
                    kdsum = Sx[6]
                    for d in range(2):
                        sg = Sx[1]; cs = Sx[2]; icl = Sx[3]; s4 = Sx[4]
                        lora2(lw2, d * 64, d * 64 + 64, wlb, "w0_%d_%d" % (d, idx), sg)
                        lora2(la2, d * 64, d * 64 + 64, alb, "a0_%d_%d" % (d, idx), icl)
                        for c in range(8):
                            scan(cs[:, c, :], onesf[:, 0:128], sg[:, c, :], [onesf, sg], [cs])
                        cp("vector", tott[:].unsqueeze(2), cs[:, :, 127:128], [cs], [tott])
                        tot = tott[:].unsqueeze(2)
                        act(pct[d][:], tott[:], AF.Exp, [tott], [pct[d]], scale=-C0)
                        S.dma(ST, PCD[d][ci], pct[d][:], [pct[d]], [bOP[d][ci]])
                        if d == 0:
                            tt("vector", sg[:], cs[:], sg[:], ALU.subtract, [cs, sg], [sg])
                            e1, e2 = cs, sg
                        else:
                            stt("vector", cs[:], cs[:], -1.0, tot.broadcast_to([128, 8, 128]), ALU.mult, ALU.add, [cs, tott], [cs])
                            tt("vector", sg[:], cs[:], sg[:], ALU.add, [cs, sg], [sg])
                            e1, e2 = sg, cs
                        a = ar[d]
                        act(e2[:], e2[:], AF.Exp, [e2], [e2], scale=-C0)
                        stt("vector", a[:, :, 0:128], kkn[:], -1.0, e2[:], ALU.mult, ALU.mult, [kkn, e2], [a])
                        act(s4[:], e1[:], AF.Exp, [e1], [s4], scale=-C0)
                        tt("vector", a[:, :, 128:256], r_t[:], s4[:], ALU.mult, [r_t, s4], [a])
                        act(e1[:], e1[:], AF.Exp, [e1], [e1], scale=C0)
                        for c in range(8):
                            act(s4[:, c, :], icl[:, c, :], AF.Identity, [icl, vecs, omka], [s4], bias=omka[:, c:c + 1], scale=vecs[:, vidx["ka_%d" % idx], c:c + 1])
                        tt("vector", s4[:], s4[:], k_t[:], ALU.mult, [s4, k_t], [s4])
                        if d == 0:
                            cp("gpsimd", kdsum[:], s4[:], [s4], [kdsum])
                        else:
                            tt("gpsimd", kdsum[:], kdsum[:], s4[:], ALU.add, [s4, kdsum], [kdsum])
                        tt("vector", kt[d][:], s4[:], e1[:], ALU.mult, [s4, e1], [kt[d]])
                        tt("gpsimd", icl[:], icl[:], kkn[:], ALU.mult, [icl, kkn], [icl])
                        tt("gpsimd", bt[d][:], icl[:], e1[:], ALU.mult, [icl, e1], [bt[d]])
                        S.dma(ST, OPA[d][ci], a[:], [a], [bOP[d][ci]])
                        S.dma(ST, OPB[d][ci], bt[d][:], [bt[d]], [bOP[d][ci]])
                        S.dma(ST, OPK[d][ci], kt[d][:], [kt[d]], [bOP[d][ci]])
                        to_tm(bt[d], TMB[d][ci], bOP[d][ci])
                        to_tm(kt[d], TMK[d][ci], bOP[d][ci])
                    bs = Sx[1]
                    tt("vector", bs[:], r_t[:], kdsum[:], ALU.mult, [r_t, kdsum], [bs])
                    tt("vector", bs[:], bs[:], bc_t(vcol("rk_%d" % idx), 128), ALU.mult, [bs, vecs], [bs])

                    def bfin(mg, pb):
                        tt("vector", bnb[:, mg * 4:(mg + 1) * 4, :], pb[:].rearrange("p (q t) -> p q t", q=4), v_t[:, mg * 4:(mg + 1) * 4, :], ALU.mult, [pb, v_t], [bnb])
                    headsum(bs, bfin)
                    S.dma(ST, fm(BN, t0, 128), bnb[:], [bnb], [bG[ci]])
                phase_end()

        def rwkv_p2(ctx_readout):
            with ExitStack() as ph:
                are_ = [sbt(ph, [128, 8, 256], BF16) for _ in range(2)]
                aro_ = [sbt(ph, [128, 8, 256], BF16) for _ in range(2)]
                for _t in are_:
                    mset("vector", _t[64:128, :, :], 0.0, [_t])
                for _t in aro_:
                    mset("vector", _t[0:64, :, :], 0.0, [_t])
                bt_ = [sbt(ph, [128, 8, 128], BF16) for _ in range(2)]
                kt_ = [sbt(ph, [128, 8, 128], BF16) for _ in range(2)]
                tmb_ = [sbt(ph, [128, D], BF16) for _ in range(2)]
                tmk_ = [sbt(ph, [128, D], BF16) for _ in range(2)]
                tmv_ = [sbt(ph, [128, D], BF16) for _ in range(2)]
                pc_ = [sbt(ph, [128, 8]) for _ in range(2)]
                ABM = sbt(ph, [128, 16, 256], BF16); AKM = sbt(ph, [128, 16, 256], BF16)
                Mb = [sbt(ph, [128, 16, 128], BF16) for _ in range(2)]
                Mtb = [sbt(ph, [128, 16, 128], BF16) for _ in range(2)]
                Ttb = [sbt(ph, [128, 16, 128], BF16) for _ in range(2)]
                MbB = [[Buf(m.t) for _ in range(4)] for m in Mb]; MtB = [[Buf(m.t) for _ in range(4)] for m in Mtb]; TtB = [[Buf(m.t) for _ in range(4)] for m in Ttb]
                ABMB = [Buf(ABM.t) for _ in range(4)]; AKMB = [Buf(AKM.t) for _ in range(4)]
                X0b = sbt(ph, [128, 16, 64], BF16); Ub = sbt(ph, [128, 16, 64], BF16)
                ysb_ = [sbt(ph, [128, D]) for _ in range(2)]
                Hf = sbt(ph, [128, 8, 64]); Hb = sbt(ph, [128, 8, 64], BF16); Htmp = sbt(ph, [128, 8, 64])
                n = 0
                for d in range(2):
                    order = list(range(NCH)) if d == 0 else ([1, 0] + list(range(NCH - 1, NCC - 1, -1)))
                    mset("vector", Hf[:], 0.0, [Hf]); mset("gpsimd", Hb[:], 0.0, [Hb])
                    for ci in order:
                        need_y = (ci >= NCC) or ctx_readout
                        are = are_[n % 2]; aro = aro_[n % 2]; arz = (are, aro); bt = bt_[n % 2]; kt = kt_[n % 2]; tmb = tmb_[n % 2]; tmk = tmk_[n % 2]; tmv = tmv_[n % 2]; pc = pc_[n % 2]
                        ysb = ysb_[n % 2]
                        n += 1
                        dep = [bOP[d][ci], bOP[0][ci]]
                        S.dma(LD, are[0:64, :, :], OPA[d][ci][0:64], dep, [are]); S.dma(LD, aro[64:128, :, :], OPA[d][ci][64:128], dep, [aro]); S.dma(LD, bt[:], OPB[d][ci], dep, [bt]); S.dma(LD, kt[:], OPK[d][ci], dep, [kt])
                        S.dma(LD, tmb[:], TMB[d][ci], dep, [tmb]); S.dma(LD, tmk[:], TMK[d][ci], dep, [tmk]); S.dma(LD, tmv[:], TMV[ci], dep, [tmv])
                        S.dma(LD, pc[:], PCD[d][ci], dep, [pc])
                        for hp in range(8):
                            for (lt, dstm, dB) in ((bt, ABM, ABMB), (kt, AKM, AKMB)):
                                pb = nextps()
                                for e_ in range(2):
                                    mm(pb[:, e_ * 256:(e_ + 1) * 256], lt[:, hp, :], arz[e_][:, hp, :], True, True, [lt, arz[e_]], [pb])
                                tt("vector", dstm[:, hp * 2:hp * 2 + 2, :], pb[:].rearrange("p (h n) -> p h n", h=2), mam[:, d, :].rearrange("p (h n) -> p h n", h=2), ALU.mult, [pb, mam], [dB[hp // 2]])
                        for hq in range(4):
                            pb = nextps()
                            for q in range(4):
                                h = hq * 4 + q
                                c = h // 2
                                mm(pb[:, q * 128:(q + 1) * 128], arz[h % 2][:, c, 0:128], bt[:, c, :], True, True, [arz[h % 2], bt], [pb])
                            tt("vector", Mb[0][:, hq * 4:hq * 4 + 4, :], pb[:].rearrange("p (h n) -> p h n", h=4), mn[:, d, :].rearrange("p (h n) -> p h n", h=4), ALU.mult, [pb, mn], [MbB[0][hq]])
                        for hq in range(4):
                            tt("gpsimd", Ttb[0][:, hq * 4:hq * 4 + 4, :], ABM[:, hq * 4:hq * 4 + 4, 0:128], idb[:].unsqueeze(1).broadcast_to([128, 4, 128]), ALU.add, [ABMB[hq], idb], [TtB[0][hq]])

                        def Mt_of(j, h):
                            return (ABM[:, h, 0:128], ABMB[h // 4]) if j == 0 else (Mtb[j % 2][:, h, :], MtB[j % 2][h // 4])
                        for j in range(1, 8):
                            for hq in range(4):
                                hs = range(hq * 4, hq * 4 + 4)
                                mprev = Mb[(j - 1) % 2]; mprevB = MbB[(j - 1) % 2][hq]
                                if j <= 6:
                                    pb = nextps()
                                    for q, h in enumerate(hs):
                                        mt, mtb = Mt_of(j - 1, h)
                                        mm(pb[:, q * 128:(q + 1) * 128], mt, mprev[:, h, :], True, True, [mtb, mprevB], [pb])
                                    cp("scalar", Mb[j % 2][:, hq * 4:hq * 4 + 4, :], pb[:].rearrange("p (h n) -> p h n", h=4), [pb], [MbB[j % 2][hq]])
                                if j <= 5:
                                    pb = nextps()
                                    for q, h in enumerate(hs):
                                        mt, mtb = Mt_of(j - 1, h)
                                        mm(pb[:, q * 128:(q + 1) * 128], mprev[:, h, :], mt, True, True, [mtb, mprevB], [pb])
                                    cp("scalar", Mtb[j % 2][:, hq * 4:hq * 4 + 4, :], pb[:].rearrange("p (h n) -> p h n", h=4), [pb], [MtB[j % 2][hq]])
                                if j >= 2:
                                    pb = nextps()
                                    src = Ttb[(j - 2) % 2]; dst = Ttb[(j - 1) % 2]
                                    srcB = TtB[(j - 2) % 2][hq]; dstB = TtB[(j - 1) % 2][hq]
                                    for q, h in enumerate(hs):
                                        mm(pb[:, q * 128:(q + 1) * 128], mprev[:, h, :], src[:, h, :], True, True, [mprevB, srcB], [pb])
                                    tt("vector", dst[:, hq * 4:hq * 4 + 4, :], pb[:].rearrange("p (h n) -> p h n", h=4), src[:, hq * 4:hq * 4 + 4, :], ALU.add, [pb, srcB], [dstB])
                        Tt = Ttb[0]
                        for hg in range(2):
                            pb = nextps()
                            for q in range(8):
                                h = hg * 8 + q; c = h // 2; po = (h % 2) * 64
                                mm(pb[:, q * 64:(q + 1) * 64], arz[h % 2][:, c, 0:128], Hb[:, c, :], True, False, [arz[h % 2], Hb], [pb])
                                mm(pb[:, q * 64:(q + 1) * 64], AKM[:, h, 0:128], tmv[:, h * 64:(h + 1) * 64], False, True, [AKMB[h // 4], tmv], [pb])
                            cp("scalar", X0b[:, hg * 8:hg * 8 + 8, :], pb[:].rearrange("p (h n) -> p h n", h=8), [pb], [X0b])
                        for hg in range(2):
                            pb = nextps()
                            for q in range(8):
                                h = hg * 8 + q
                                mm(pb[:, q * 64:(q + 1) * 64], Tt[:, h, :], X0b[:, h, :], True, True, [TtB[0][h // 4], X0b], [pb])
                            cp("vector", Ub[:, hg * 8:hg * 8 + 8, :], pb[:].rearrange("p (h n) -> p h n", h=8), [pb], [Ub])
                        if need_y:
                            for hg in range(2):
                                pb = nextps()
                                for q in range(8):
                                    h = hg * 8 + q; c = h // 2; po = (h % 2) * 64
                                    mm(pb[:, q * 64:(q + 1) * 64], arz[h % 2][:, c, 128:256], Hb[:, c, :], True, False, [arz[h % 2], Hb], [pb])
                                    mm(pb[:, q * 64:(q + 1) * 64], ABM[:, h, 128:256], Ub[:, h, :], False, False, [ABMB[h // 4], Ub], [pb])
                                    mm(pb[:, q * 64:(q + 1) * 64], AKM[:, h, 128:256], tmv[:, h * 64:(h + 1) * 64], False, True, [AKMB[h // 4], tmv], [pb])
                                cp("scalar", ysb[:, hg * 512:(hg + 1) * 512], pb[:], [pb], [ysb])
                            S.dma(ST, Y[d][ci], ysb[:], [ysb], [bY[d][ci]])
                        pbs = (nextps(), nextps())
                        for h in range(16):
                            c = h // 2; po = (h % 2) * 64
                            pb = pbs[h % 2]
                            mm(pb[po:po + 64, c * 64:(c + 1) * 64], tmb[:, h * 64:(h + 1) * 64], Ub[:, h, :], True, False, [tmb, Ub], [pb])
                            mm(pb[po:po + 64, c * 64:(c + 1) * 64], tmk[:, h * 64:(h + 1) * 64], tmv[:, h * 64:(h + 1) * 64], False, True, [tmk, tmv], [pb])
                        for e_ in range(2):
                            po = e_ * 64
                            tt("vector", Htmp[po:po + 64, :, :], pbs[e_][po:po + 64, :].rearrange("p (c n) -> p c n", c=8), Hf[po:po + 64, :, :], ALU.add, [pbs[e_], Hf], [Htmp])
                        tt("vector", Hf[:], Htmp[:], pc[:].unsqueeze(2).broadcast_to([128, 8, 64]), ALU.mult, [Htmp, pc], [Hf])
                        cp("scalar", Hb[:], Hf[:], [Hf], [Hb])
                phase_end()

        def rwkv_p3(i, idx, ctx_readout):
            with ExitStack() as ph:
                wo = sbt(ph, [128, 8, D], BF16)
                with ExitStack() as sp:
                    stg = [sbt(sp, [128, 4096]) for _ in range(2)]
                    load_w(stg, wo, wo_in[idx], 128, D)
                    phase_end()
                WT = 512
                xt_ = [sbt(ph, [128, 8, WT]) for _ in range(2)]
                gt = sbt(ph, [128, 8, WT], BF16); bnt = sbt(ph, [128, 8, WT], BF16)
                pre = sbt(ph, [128, 8, WT]); preb = sbt(ph, [128, 8, WT], BF16)
                y0_ = [sbt(ph, [128, D]) for _ in range(2)]; y1_ = [sbt(ph, [128, D]) for _ in range(2)]
                ysq = sbt(ph, [128, D]); yn = sbt(ph, [128, D], BF16)
                st = sbt(ph, [128, 4, 16])
                tiles = ([(0, NCC)] if ctx_readout else []) + [(c, 4) for c in range(NCC, NCH, 4)]
                n = 0
                for tn, (c0, nch) in enumerate(tiles):
                    kind = chunk_kind(c0)
                    w = nch * 128
                    xt = xt_[tn % 2]
                    xb = [bX[c0 + j] for j in range(nch)]
                    S.dma(LD, xt[:, :, 0:w], fm(X, c0 * 128, w), xb, [xt])
                    S.dma(LD, gt[:, :, 0:w], fm(G, c0 * 128, w), [bG[c0 + j] for j in range(nch)], [gt])
                    S.dma(LD, bnt[:, :, 0:w], fm(BN, c0 * 128, w), [bG[c0 + j] for j in range(nch)], [bnt])
                    for j in range(nch):
                        ci = c0 + j
                        y0 = y0_[n % 2]; y1 = y1_[n % 2]
                        n += 1
                        S.dma(LD, y0[:], Y[0][ci], [bY[0][ci]], [y0]); S.dma(LD, y1[:], Y[1][ci], [bY[1][ci]], [y1])
                        tt("vector", y0[:], y0[:], y1[:], ALU.add, [y0, y1], [y0])
                        y3 = y0[:].rearrange("p (h n) -> p h n", h=16)
                        rsum("vector", st[:, 0, :], y3, [y0], [st])
                        tt("gpsimd", ysq[:], y0[:], y0[:], ALU.mult, [y0], [ysq])
                        rsum("vector", st[:, 1, :], ysq[:].rearrange("p (h n) -> p h n", h=16), [ysq], [st])
                        ts("vector", st[:, 0, :], st[:, 0, :], 1.0 / 64, ALU.mult, [st], [st])
                        tt("vector", st[:, 2, :], st[:, 0, :], st[:, 0, :], ALU.mult, [st], [st])
                        stt("vector", st[:, 1, :], st[:, 1, :], 1.0 / 64, st[:, 2, :], ALU.mult, ALU.subtract, [st], [st])
                        act(st[:, 1, :], st[:, 1, :], AF.Sqrt, [st, kst], [st], bias=kst[:, 1:2])
                        recip(st[:, 1, :], st[:, 1, :], [st], [st])
                        tt("vector", y3, y3, st[:, 0, :].unsqueeze(2).broadcast_to([128, 16, 64]), ALU.subtract, [y0, st], [y0])
                        tt("vector", yn[:].rearrange("p (h n) -> p h n", h=16), y3, st[:, 1, :].unsqueeze(2).broadcast_to([128, 16, 64]), ALU.mult, [y0, st], [yn])
                        pb = nextps()
                        pbb = pb[:].bitcast(BF16)
                        for c in range(8):
                            tr(pbb[:, c * 128:(c + 1) * 128], yn[:, c * 128:(c + 1) * 128], idb[:], [yn, idb], [pb])
                        for c in range(8):
                            act(pre[:, c, j * 128:(j + 1) * 128], pbb[:, c * 128:(c + 1) * 128], AF.Identity, [pb, vecs], [pre],
                                bias=vecs[:, vidx["lb_%d" % idx], c:c + 1], scale=vecs[:, vidx["lw_%d" % idx], c:c + 1])
                    tt("vector", pre[:, :, 0:w], pre[:, :, 0:w], bnt[:, :, 0:w], ALU.add, [pre, bnt], [pre])
                    tt("gpsimd", preb[:, :, 0:w], pre[:, :, 0:w], gt[:, :, 0:w], ALU.mult, [pre, gt], [preb])
                    for m in range(8):
                        pb = nextps()
                        for kc in range(8):
                            mm(pb[:, 0:w], wo[:, kc, m * 128:(m + 1) * 128], preb[:, kc, 0:w], kc == 0, kc == 7, [wo, preb], [pb])
                        stt("vector", xt[:, m, 0:w], pb[:, 0:w], lay[:, kind, 2, m:m + 1], xt[:, m, 0:w], ALU.mult, ALU.add, [pb, lay, xt], [xt])
                    S.dma(ST, fm(X, c0 * 128, w), xt[:, :, 0:w], [xt], xb)
                phase_end()

        def fnet_f1(do_ctx):
            with ExitStack() as ph:
                dftc = sbt(ph, [128, 2, 512], BF16)
                S.dma(LD, dftc[:], dftc_in, (), [dftc])
                xt_ = [sbt(ph, [128, 8, 128]) for _ in range(2)]
                hb = sbt(ph, [128, 8, 128], BF16)
                ycs_ = [sbt(ph, [128, 2, D], BF16) for _ in range(2)]
                sq_ = [sbt(ph, [128, 128]) for _ in range(2)]; rstd = sbt(ph, [128, 128])
                for ci in range(0 if do_ctx else NCC, NCH):
                    kind = chunk_kind(ci)
                    xt = xt_[ci % 2]; ycs = ycs_[ci % 2]
                    S.dma(LD, xt[:], fm(X, ci * 128, 128), [bX[ci]], [xt])
                    norm_mod(xt, 128, kind, 0, sq_, rstd, hb)
                    for g in range(4):
                        pb = nextps()
                        for kc in range(2):
                            mm(pb[:], hb[:, 2 * g + kc, :], dftc[:, kc, :], kc == 0, kc == 1, [hb, dftc], [pb])
                        cp("vector" if g % 2 == 0 else "scalar", ycs[:, :, g * 256:(g + 1) * 256], pb[:].rearrange("p (a n) -> p a n", a=2), [pb], [ycs])
                    S.dma(ST, YCS[ci], ycs[:], [ycs], [bYCS[ci]])
                phase_end()

        def fnet_f2(i, idx, do_ctx):
            with ExitStack() as ph:
                wo = sbt(ph, [128, 8, D], BF16)
                with ExitStack() as sp:
                    stg = [sbt(sp, [128, 4096]) for _ in range(2)]
                    load_w(stg, wo, fwo_in[idx], 128, D)
                    phase_end()
                WT = 512
                ycs_ = [sbt(ph, [128, 2, D], BF16) for _ in range(3)]
                dl_ = [sbt(ph, [128, 2, WT], BF16) for _ in range(3)]
                fb = sbt(ph, [128, 8, WT], BF16)
                xt_ = [sbt(ph, [128, 8, WT]) for _ in range(2)]
                jobs = []
                if do_ctx:
                    jobs.append((1, 0, NCC, dlc_in, 0, CTX))
                for k0 in range(0, SEQ, WT):
                    jobs.append((0, NCC, NCH - NCC, dll_in, k0, WT))
                n = 0
                for tn, (kind, cbase, nl, DL, k0, wk) in enumerate(jobs):
                    xt = xt_[tn % 2]
                    tok0 = cbase * 128 + k0
                    xb = [bX[tok0 // 128 + j] for j in range(wk // 128)]
                    S.dma(LD, xt[:, :, 0:wk], fm(X, tok0, wk), xb, [xt])
                    acc = [nextps() for _ in range(8)]
                    for li in range(nl):
                        ycs = ycs_[n % 3]; dl = dl_[n % 3]
                        n += 1
                        S.dma(LD, ycs[:], YCS[cbase + li], [bYCS[cbase + li]], [ycs])
                        S.dma(LD, dl[:, :, 0:wk], DL[:, li * 128:(li + 1) * 128, k0:k0 + wk].rearrange("a p k -> p a k"), (), [dl])
                        for m in range(8):
                            mm(acc[m][:, 0:wk], ycs[:, 0, m * 128:(m + 1) * 128], dl[:, 0, 0:wk], li == 0, False, [ycs, dl], [acc[m]])
                            mm(acc[m][:, 0:wk], ycs[:, 1, m * 128:(m + 1) * 128], dl[:, 1, 0:wk], False, li == nl - 1, [ycs, dl], [acc[m]])
                    for m in range(8):
                        cp("vector" if m % 2 == 0 else "scalar", fb[:, m, 0:wk], acc[m][:, 0:wk], [acc[m]], [fb])
                    for m in range(8):
                        pb = nextps()
                        for kc in range(8):
                            mm(pb[:, 0:wk], wo[:, kc, m * 128:(m + 1) * 128], fb[:, kc, 0:wk], kc == 0, kc == 7, [wo, fb], [pb])
                        stt("vector", xt[:, m, 0:wk], pb[:, 0:wk], lay[:, kind, 2, m:m + 1], xt[:, m, 0:wk], ALU.mult, ALU.add, [pb, lay, xt], [xt])
                    S.dma(ST, fm(X, tok0, wk), xt[:, :, 0:wk], [xt], xb)
                phase_end()

        def final_phase():
            with ExitStack() as ph:
                xt_ = [sbt(ph, [128, 8, 128]) for _ in range(2)]
                ot_ = [sbt(ph, [128, D]) for _ in range(2)]
                sq_ = [sbt(ph, [128, 128]) for _ in range(2)]; rstd = sbt(ph, [128, 128])
                for ci in range(NCC, NCH):
                    xt = xt_[ci % 2]; ot = ot_[ci % 2]
                    S.dma(LD, xt[:], fm(X, ci * 128, 128), [bX[ci]], [xt])
                    rms_rstd(xt, 128, sq_, rstd)
                    tt("vector", xt[:], xt[:], rstd[:, 0:128].unsqueeze(1).broadcast_to([128, 8, 128]), ALU.mult, [xt, rstd], [xt])
                    tt("gpsimd", xt[:], xt[:], bc_t(vcol("fin"), 128), ALU.mult, [xt, vecs], [xt])
                    for hf in range(2):
                        pb = nextps()
                        for q in range(4):
                            c = hf * 4 + q
                            tr(pb[:, q * 128:(q + 1) * 128], xt[:, c, :], idf[:], [xt, idf], [pb])
                        cp("vector" if hf == 0 else "scalar", ot[:, hf * 512:(hf + 1) * 512], pb[:], [pb], [ot])
                    S.dma(ST, out[(ci - NCC) * 128:(ci - NCC + 1) * 128, :], ot[:], [ot], ())
                phase_end()

        for i in range(DEPTH):
            mixer = i % 2
            idx = i // 2
            ctx_in = i <= last_a
            ctx_out = i < last_a
            set_layer(i)
            on = lambda p: phases is None or p in phases
            if mixer == 0:
                if on("p1"): rwkv_p1(i, idx)
                if on("p2"): rwkv_p2(ctx_out)
                if on("p3"): rwkv_p3(i, idx, ctx_out)
            else:
                if on("f1"): fnet_f1(ctx_out)
                if on("f2"): fnet_f2(i, idx, ctx_out)
            if on("mlp"): mlp_phase(i, ctx_out)
        import os
        if not os.environ.get('K_SKIPFINAL'):
            final_phase()
        print("instructions (incl waits):", S.ninst)
    return nc


def _col(v):
    return np.ascontiguousarray(np.asarray(v, np.float32).reshape(8, 128).T)


def host_consts(SEQ, CTX):
    bf = ml_dtypes.bfloat16
    c = {}
    c["idb"] = np.eye(128, dtype=np.float32).astype(bf)
    c["idf"] = np.eye(128, dtype=np.float32)
    c["onesf"] = np.ones((128, 128), np.float32)
    blk = np.zeros((128, 128), np.float32); blk[:64, :64] = 1; blk[64:, 64:] = 1
    c["blk"] = blk
    s = np.arange(128)[:, None]; t = np.arange(128)[None, :]
    mam = np.zeros((2, 128, 512), np.float32)
    for d, (st, inc) in enumerate((((s < t), (s <= t)), ((s > t), (s >= t)))):
        one = np.concatenate([st, inc], 1).astype(np.float32)
        mam[d] = np.concatenate([one, one], 1)
    c["mam"] = mam.astype(bf)
    mn = np.zeros((2, 128, 512), np.float32)
    mn[0] = np.tile((s > t).astype(np.float32), (1, 4))
    mn[1] = np.tile((s < t).astype(np.float32), (1, 4))
    c["mn"] = mn.astype(bf)
    cc = np.arange(256)
    ang = 2 * np.pi * np.outer(cc, cc) / 256.0
    dftc = np.concatenate([np.cos(ang), np.sin(ang)], 1) / 16.0
    c["dftc"] = np.ascontiguousarray(dftc.reshape(2, 128, 512).transpose(1, 0, 2)).astype(bf)

    def dl(L):
        l = np.arange(L, dtype=np.int64)
        prod = np.outer(l, l) % L
        ang = 2 * np.pi * prod / float(L)
        return np.stack([np.cos(ang), -np.sin(ang)]).astype(np.float32) / math.sqrt(L)
    c["dll"] = dl(SEQ).astype(bf)
    c["dlc"] = dl(CTX).astype(bf)
    return c


def host_shared(inp, DEPTH):
    f = lambda a: np.ascontiguousarray(np.asarray(a, np.float32))
    n_a = (DEPTH + 1) // 2
    vidx = vec_index(DEPTH)
    vecs = np.zeros((128, len(vidx), 8), np.float32)

    def put(name, v):
        vecs[:, vidx[name], :] = _col(v)
    for i in range(DEPTH):
        put("n1_%d" % i, inp["norm1_g"][i]); put("n2_%d" % i, inp["norm2_g"][i])
    put("fin", inp["final_g"])
    for a in range(n_a):
        for j in range(6):
            put("mu%d_%d" % (j, a), inp["rk_mu"][a, j])
        for d in range(2):
            put("w0_%d_%d" % (d, a), inp["rk_w0"][a, d]); put("a0_%d_%d" % (d, a), inp["rk_a0"][a, d])
        put("kk_%d" % a, inp["rk_kk"][a]); put("ka_%d" % a, inp["rk_ka"][a]); put("rk_%d" % a, np.asarray(inp["rk_rk"][a]).reshape(-1))
        put("lw_%d" % a, inp["rk_lnx_w"][a]); put("lb_%d" % a, inp["rk_lnx_b"][a])
        if a >= 1:
            put("v0_%d" % a, inp["rk_v0"][a - 1])
    sh = {}
    sh["vecs"] = vecs.reshape(128, -1)
    sh["modw"] = f(inp["mod_w"][:DEPTH]); sh["modb"] = np.ascontiguousarray(np.asarray(inp["mod_b"][:DEPTH], np.float32).reshape(DEPTH, 48, 128).transpose(2, 0, 1).reshape(128, DEPTH * 48))
    sh["w1m"] = f(inp["mlp_w1"][:DEPTH]); sh["w2m"] = f(inp["mlp_w2"][:DEPTH])
    sh["wr"] = f(inp["rk_wr"][:n_a]); sh["wk"] = f(inp["rk_wk"][:n_a]); sh["wv"] = f(inp["rk_wv"][:n_a]); sh["wo"] = f(inp["rk_wo"][:n_a])
    cat1 = lambda w: f(np.concatenate([np.asarray(w)[:n_a, 0], np.asarray(w)[:n_a, 1]], axis=2))
    cat2 = lambda w: f(np.concatenate([np.asarray(w)[:n_a, 0], np.asarray(w)[:n_a, 1]], axis=1))
    sh["lw1"] = cat1(inp["rk_w1"]); sh["lw2"] = cat2(inp["rk_w2"]); sh["la1"] = cat1(inp["rk_a1"]); sh["la2"] = cat2(inp["rk_a2"])
    sh["lg1"] = f(inp["rk_g1"][:n_a]); sh["lg2"] = f(inp["rk_g2"][:n_a])
    if n_a >= 2:
        sh["lv1"] = f(inp["rk_v1"][:n_a - 1]); sh["lv2"] = f(inp["rk_v2"][:n_a - 1])
    else:
        sh["lv1"] = np.zeros((1, D, 32), np.float32); sh["lv2"] = np.zeros((1, 32, D), np.float32)
    n_b = DEPTH // 2
    sh["fwo"] = f(inp["ft_wo"][:n_b]) if n_b >= 1 else np.zeros((1, D, D), np.float32)
    return sh


def run(inp, SEQ, CTX, DEPTH, n_cores, dbg=False, phases=None):
    nc = build(SEQ, CTX, DEPTH, dbg, phases)
    consts = host_consts(SEQ, CTX)
    sh = host_shared(inp, DEPTH)
    in_maps = []
    for b in range(n_cores):
        m = dict(consts); m.update(sh)
        m["xin"] = np.ascontiguousarray(np.asarray(inp["x"][b], np.float32))
        m["cin"] = np.ascontiguousarray(np.asarray(inp["ctx"][b], np.float32))
        ccv = np.zeros((128, 8, 2), np.float32)
        ccv[:, :, 0] = _col(inp["c"][b]); ccv[:, :, 1] = _col(inp["c_ctx"])
        m["cc"] = ccv.reshape(128, 16)
        in_maps.append(m)
    res = run_bass_kernel_spmd(nc, in_maps, core_ids=list(range(n_cores)))
    return res


def kernel(**inputs):
    x = np.asarray(inputs["x"])
    B, SEQ, _ = x.shape
    CTX = np.asarray(inputs["ctx"]).shape[1]
    DEPTH = np.asarray(inputs["mod_w"]).shape[0]
    res = run(inputs, SEQ, CTX, DEPTH, B)
    return np.stack([np.asarray(r["out"], np.float32) for r in res.results], axis=0)
```
